# Optimizing a Trainium2 kernel written in Bass

```python
import math
import jax, jax.numpy as jnp
from jax import lax
import numpy as np

D_MODEL = 2048
BATCH = 8
SEQ = 2048
DEPTH = 1

GRID_W = 64
NA_HEADS = 8
NA_HEAD_DIM = 128
NA_WIN_ROWS = 8
NA_WIN_COLS = 16
NA_WIDTH = NA_HEADS * NA_HEAD_DIM
MLA_HEADS = 8
MLA_Q_RANK = 512
MLA_KV_RANK = 512
MLA_NOPE_DIM = 128
MLA_ROPE_DIM = 64
MLA_V_DIM = 128
MLA_QK_DIM = MLA_NOPE_DIM + MLA_ROPE_DIM
MLA_WIDTH = MLA_HEADS * MLA_V_DIM
ROPE_THETA = 10000.0
Q_BLOCK = 128
D_FF = 5632
PL_DIM = 256
NORM_EPS = 1e-6
NEG_INF = -1e30
IN_SIZES = (NA_WIDTH, NA_WIDTH, NA_WIDTH, MLA_Q_RANK, MLA_KV_RANK, MLA_ROPE_DIM, D_MODEL, D_MODEL)
N_IN = NA_WIDTH * 3 + MLA_Q_RANK + MLA_KV_RANK + MLA_ROPE_DIM + 2 * D_MODEL

kernel_name = "hybrid_na2d_mla_macaron_encoder"


def rmsnorm(x, g):
    xf = x.astype(jnp.float32)
    y = xf * lax.rsqrt(jnp.mean(xf * xf, axis=-1, keepdims=True) + NORM_EPS)
    return (y * g.astype(jnp.float32)).astype(x.dtype)


def swiglu(x, w_gate, w_up, w_down):
    return (jax.nn.silu(x @ w_gate) * (x @ w_up)) @ w_down


def split_points():
    pts, acc = [], 0
    for s in IN_SIZES[:-1]:
        acc += s
        pts.append(acc)
    return pts


def rope_tables(seq_len, dim):
    pos = jnp.arange(seq_len, dtype=jnp.float32)
    inv_freq = 1.0 / (ROPE_THETA ** (jnp.arange(0, dim, 2, dtype=jnp.float32) / dim))
    ang = pos[:, None] * inv_freq[None, :]
    return jnp.cos(ang), jnp.sin(ang)


def apply_rope(x, cos, sin):
    half = x.shape[-1] // 2
    x1, x2 = x[..., :half], x[..., half:]
    cos = cos.astype(x.dtype)
    sin = sin.astype(x.dtype)
    return jnp.concatenate([x1 * cos - x2 * sin, x2 * cos + x1 * sin], axis=-1)


def neighbourhood_attention(q, k, v, rpb):
    B, S, _ = q.shape
    rows = S // GRID_W
    kh = min(NA_WIN_ROWS, rows)
    kw = NA_WIN_COLS
    grid = lambda t: t.reshape(B, rows, GRID_W, NA_HEADS, NA_HEAD_DIM)
    qg, kg, vg = grid(q), grid(k), grid(v)
    cols = np.arange(GRID_W)
    col_start = np.clip(cols - kw // 2, 0, GRID_W - kw)
    col_mask = (cols[None, :] >= col_start[:, None]) & (cols[None, :] < col_start[:, None] + kw)
    dc_idx = np.clip(cols[None, :] - cols[:, None], -(kw - 1), kw - 1) + (kw - 1)
    col_mask = jnp.asarray(col_mask)[:, None, :]
    dc_idx = jnp.asarray(dc_idx)
    scale = NA_HEAD_DIM ** -0.5

    def row_block(args):
        q_row, r = args
        rs = jnp.clip(r - kh // 2, 0, rows - kh)
        k_rows = lax.dynamic_slice_in_dim(kg, rs, kh, axis=1)
        v_rows = lax.dynamic_slice_in_dim(vg, rs, kh, axis=1)
        dr = rs + jnp.arange(kh) - r
        bias = rpb[:, dr + (NA_WIN_ROWS - 1)][:, :, dc_idx]
        bias = bias.transpose(0, 2, 1, 3).astype(jnp.float32)
        s = jnp.einsum('bqhd,bikhd->bhqik', q_row, k_rows).astype(jnp.float32) * scale + bias
        s = jnp.where(col_mask, s, NEG_INF)
        pr = jax.nn.softmax(s.reshape(B, NA_HEADS, GRID_W, kh * GRID_W), axis=-1)
        pr = pr.reshape(s.shape).astype(v.dtype)
        return jnp.einsum('bhqik,bikhd->bqhd', pr, v_rows)

    o = lax.map(row_block, (qg.swapaxes(0, 1), jnp.arange(rows)))
    return o.swapaxes(0, 1).reshape(B, S, NA_WIDTH)


def mla_attention(q_lat, kv_lat, k_rope_in, q_a_norm, w_uq, kv_a_norm, w_ukv):
    B, S, _ = q_lat.shape
    cq = rmsnorm(q_lat, q_a_norm)
    q = (cq @ w_uq).reshape(B, S, MLA_HEADS, MLA_QK_DIM)
    q_nope, q_rope = q[..., :MLA_NOPE_DIM], q[..., MLA_NOPE_DIM:]
    ckv = rmsnorm(kv_lat, kv_a_norm)
    kv = (ckv @ w_ukv).reshape(B, S, MLA_HEADS, MLA_NOPE_DIM + MLA_V_DIM)
    k_nope, v = kv[..., :MLA_NOPE_DIM], kv[..., MLA_NOPE_DIM:]
    cos, sin = rope_tables(S, MLA_ROPE_DIM)
    q_rope = apply_rope(q_rope, cos[:, None, :], sin[:, None, :])
    k_rope = apply_rope(k_rope_in, cos, sin)
    scale = MLA_QK_DIM ** -0.5
    nb = S // Q_BLOCK
    to_blocks = lambda t: t.reshape(B, nb, Q_BLOCK, *t.shape[2:]).swapaxes(0, 1)

    def q_block(args):
        qn, qr = args
        s = jnp.einsum('bqhd,bkhd->bhqk', qn, k_nope) + jnp.einsum('bqhr,bkr->bhqk', qr, k_rope)
        pr = jax.nn.softmax(s.astype(jnp.float32) * scale, axis=-1).astype(v.dtype)
        return jnp.einsum('bhqk,bkhd->bqhd', pr, v)

    o = lax.map(q_block, (to_blocks(q_nope), to_blocks(q_rope)))
    return o.swapaxes(0, 1).reshape(B, S, MLA_WIDTH)


def setup_inputs(seed: int = 0) -> dict:
    key = jax.random.key(seed)
    ks = iter(jax.random.split(key, 32))
    f32 = jnp.float32
    w = lambda shape, fan_in: jax.random.normal(next(ks), shape, f32) * (fan_in ** -0.5)
    gain = lambda shape: 1.0 + 0.01 * jax.random.normal(next(ks), shape, f32)
    L = DEPTH
    return {
        "x": jax.random.normal(next(ks), (BATCH, SEQ, D_MODEL), f32),
        "p": jax.random.normal(next(ks), (DEPTH, BATCH, SEQ, PL_DIM), f32),
        "ffn1_norm": gain((L, D_MODEL)),
        "ffn1_w_gate": w((L, D_MODEL, D_FF), D_MODEL),
        "ffn1_w_up": w((L, D_MODEL, D_FF), D_MODEL),
        "ffn1_w_down": w((L, D_FF, D_MODEL), D_FF),
        "mix_norm": gain((L, D_MODEL)),
        "w_in": w((L, D_MODEL, N_IN), D_MODEL),
        "q_a_norm": gain((L, MLA_Q_RANK)),
        "w_uq": w((L, MLA_Q_RANK, MLA_HEADS * MLA_QK_DIM), MLA_Q_RANK),
        "kv_a_norm": gain((L, MLA_KV_RANK)),
        "w_ukv": w((L, MLA_KV_RANK, MLA_HEADS * (MLA_NOPE_DIM + MLA_V_DIM)), MLA_KV_RANK),
        "na_rpb": 0.02 * jax.random.normal(next(ks), (L, NA_HEADS, 2 * NA_WIN_ROWS - 1, 2 * NA_WIN_COLS - 1), f32),
        "w_branch_a": w((L, NA_WIDTH, D_MODEL), NA_WIDTH),
        "w_branch_b": w((L, MLA_WIDTH, D_MODEL), MLA_WIDTH),
        "w_out": w((L, D_MODEL, D_MODEL), D_MODEL),
        "ffn2_norm": gain((L, D_MODEL)),
        "ffn2_w_gate": w((L, D_MODEL, D_FF), D_MODEL),
        "ffn2_w_up": w((L, D_MODEL, D_FF), D_MODEL),
        "ffn2_w_down": w((L, D_FF, D_MODEL), D_FF),
        "pl_norm": gain((L, D_MODEL)),
        "w_pl": w((L, PL_DIM, D_MODEL), PL_DIM),
        "w_pl_gate": w((L, D_MODEL, D_MODEL), D_MODEL),
        "final_norm": gain((D_MODEL,)),
    }


def reference(x, p, ffn1_norm, ffn1_w_gate, ffn1_w_up, ffn1_w_down, mix_norm, w_in,
              q_a_norm, w_uq, kv_a_norm, w_ukv, na_rpb, w_branch_a, w_branch_b, w_out,
              ffn2_norm, ffn2_w_gate, ffn2_w_up, ffn2_w_down, pl_norm, w_pl, w_pl_gate,
              final_norm):
    pts = split_points()
    h = x
    for i in range(DEPTH):
        h = h + 0.5 * swiglu(rmsnorm(h, ffn1_norm[i]), ffn1_w_gate[i], ffn1_w_up[i], ffn1_w_down[i])
        u = rmsnorm(h, mix_norm[i])
        z = u @ w_in[i]
        na_q, na_k, na_v, q_lat, kv_lat, k_rope, gate_a, gate_b = jnp.split(z, pts, axis=-1)
        y_a = neighbourhood_attention(na_q, na_k, na_v, na_rpb[i]) @ w_branch_a[i]
        y_b = mla_attention(q_lat, kv_lat, k_rope, q_a_norm[i], w_uq[i], kv_a_norm[i], w_ukv[i]) @ w_branch_b[i]
        merged = jax.nn.sigmoid(gate_a) * y_a + jax.nn.sigmoid(gate_b) * y_b
        h = h + merged @ w_out[i]
        h = h + 0.5 * swiglu(rmsnorm(h, ffn2_norm[i]), ffn2_w_gate[i], ffn2_w_up[i], ffn2_w_down[i])
        pl_gate = jax.nn.sigmoid(rmsnorm(h, pl_norm[i]) @ w_pl_gate[i])
        h = h + pl_gate * (p[i] @ w_pl[i])
    return rmsnorm(h, final_norm)
```

```python
import contextlib
import numpy as np
import concourse.bass as bass
import concourse.mybir as mybir
from concourse.bass_utils import run_bass_kernel_spmd

F32 = mybir.dt.float32
BF16 = mybir.dt.bfloat16
AF = mybir.ActivationFunctionType
ALU = mybir.AluOpType

D = 2048
S = 2048
DFF = 5632
NH = 8
EPS = 1e-6
QNAMES = ("pe", "act", "dve", "pool", "sp")
ALL_PHASES = ("ffn1", "proj", "mla", "na", "merge", "ffn2", "pl")


class Buf:
    __slots__ = ("ap", "w", "r", "excl")

    def __init__(self, ap=None, excl=False):
        self.ap = ap
        self.w = []
        self.r = {}
        self.excl = excl


class Prog:
    def __init__(self, nc):
        self.nc = nc
        self.q = {n: [] for n in QNAMES}
        self.dma_count = {}
        self.pending = {n: [] for n in QNAMES}

    @staticmethod
    def _split(reads, writes):
        ex = [b for b in reads if b.excl]
        if ex:
            reads = [b for b in reads if not b.excl]
            writes = list(writes) + ex
        return reads, writes

    def _deps(self, reads, writes):
        deps = []
        for b in reads:
            deps += b.w
        for b in writes:
            deps += b.w
            deps += list(b.r.values())
        return deps

    def _mark(self, ev, reads, writes):
        for b in reads:
            if ev[0] == "e":
                b.r[ev[1]] = ev
            else:
                b.r[("d", ev[1])] = ev
        for b in writes:
            b.w = [ev]
            b.r = {}

    def op(self, q, fn, reads=(), writes=(), extra=()):
        reads, writes = self._split(reads, writes)
        deps = self._deps(reads, writes) + list(extra) + self.pending[q]
        self.pending[q] = []
        ev = ("e", q, len(self.q[q]))
        self.q[q].append(dict(fn=fn, deps=deps, signal=False, dma=None))
        self._mark(ev, reads, writes)
        return ev

    def dma(self, q, semkey, fn, reads=(), writes=(), extra=()):
        reads, writes = self._split(reads, writes)
        deps = self._deps(reads, writes) + list(extra) + self.pending[q]
        self.pending[q] = []
        c = self.dma_count.get(semkey, 0) + 16
        self.dma_count[semkey] = c
        ev = ("d", semkey, c)
        self.q[q].append(dict(fn=fn, deps=deps, signal=False, dma=semkey))
        self._mark(ev, reads, writes)
        return ev

    def barrier(self):
        evs = []
        for n in QNAMES:
            if self.q[n] and n != "sp":
                evs.append(("e", n, len(self.q[n]) - 1))
        for k, c in self.dma_count.items():
            evs.append(("d", k, c))
        for n in QNAMES:
            self.pending[n] = self.pending[n] + evs

    def finish(self):
        self.barrier()
        for n in QNAMES:
            if n != "sp":
                self.pending[n] = []
        self.op("sp", lambda e: e.nop())

    def emit(self):
        nc = self.nc
        for n in QNAMES:
            seen_e = {m: -1 for m in QNAMES}
            seen_d = {}
            for i, o in enumerate(self.q[n]):
                best = {}
                for ev in o["deps"]:
                    if ev[0] == "e":
                        _, m, idx = ev
                        if m == n and n in ("pe", "sp"):
                            continue
                        if idx <= seen_e[m]:
                            continue
                    else:
                        if ev[2] <= seen_d.get(ev[1], 0):
                            continue
                    key = (ev[0], ev[1])
                    if key not in best or ev[2] > best[key][2]:
                        best[key] = ev
                o["kept"] = list(best.values())
                for ev in o["kept"]:
                    if ev[0] == "e":
                        seen_e[ev[1]] = ev[2]
                        self.q[ev[1]][ev[2]]["signal"] = True
                    else:
                        seen_d[ev[1]] = ev[2]
        val = {}
        for n in QNAMES:
            c = 0
            v = []
            for o in self.q[n]:
                if o["signal"]:
                    c += 1
                v.append(c)
            val[n] = v
        self.stats = {n: (len(self.q[n]), val[n][-1] if val[n] else 0,
                          sum(len(o["kept"]) for o in self.q[n])) for n in QNAMES}
        with contextlib.ExitStack() as st:
            esem = {n: st.enter_context(nc.semaphore("s_" + n)) for n in QNAMES}
            dsem = {k: st.enter_context(nc.semaphore("d_%s" % (k,))) for k in self.dma_count}
            block = st.enter_context(nc.Block())

            def run(n, eng):
                for o in self.q[n]:
                    for ev in o["kept"]:
                        if ev[0] == "e":
                            eng.wait_ge(esem[ev[1]], val[ev[1]][ev[2]])
                        else:
                            eng.wait_ge(dsem[ev[1]], ev[2])
                    inst = o["fn"](eng)
                    if o["dma"] is not None:
                        inst.then_inc(dsem[o["dma"]], 16)
                    elif o["signal"]:
                        inst.then_inc(esem[n], 1)

            @block.tensor
            def _(eng):
                run("pe", eng)

            @block.scalar
            def _(eng):
                run("act", eng)

            @block.vector
            def _(eng):
                run("dve", eng)

            @block.gpsimd
            def _(eng):
                run("pool", eng)

            @block.sync
            def _(eng):
                run("sp", eng)


def lay(W, nb=128):
    K, N = W.shape
    return np.ascontiguousarray(W.reshape(K // 128, 128, N // nb, nb).transpose(2, 1, 0, 3))


def gain_cols(g):
    return np.ascontiguousarray(g.reshape(-1, 128).T)


def rot64(W):
    K, N = W.shape
    return W.reshape(K, N // 64, 2, 32)[:, :, ::-1, :].reshape(K, N)


def na_geometry():
    rows, Wd, kh, kw = 32, 64, 8, 16
    r = np.arange(rows)
    rs = np.clip(r - kh // 2, 0, rows - kh)
    c = np.arange(Wd)
    cs = np.clip(c - kw // 2, 0, Wd - kw)
    tiles = []
    geoms = {}
    glist = []
    for qt in range(4):
        qr = qt * 8 + np.arange(8)
        for j in range(16):
            kr = 2 * j + np.arange(2)
            rowok = (kr[:, None] >= rs[qr][None, :]) & (kr[:, None] < rs[qr][None, :] + kh)
            if not rowok.any():
                continue
            dr = kr[:, None] - qr[None, :] + (8 - 1)
            colok = (c[:, None] >= cs[None, :]) & (c[:, None] < cs[None, :] + kw)
            dc = np.clip(c[:, None] - c[None, :], -(kw - 1), kw - 1) + (kw - 1)
            mask = (rowok[:, None, :, None] & colok[None, :, None, :]).reshape(128, 512)
            dri = np.broadcast_to(np.clip(dr, 0, 14)[:, None, :, None], (2, 64, 8, 64)).reshape(128, 512)
            dci = np.broadcast_to(dc[None, :, None, :], (2, 64, 8, 64)).reshape(128, 512)
            key = (mask.tobytes(), (dri * mask).tobytes())
            if key not in geoms:
                geoms[key] = len(glist)
                glist.append((dri.copy(), dci.copy(), mask.copy()))
            tiles.append((qt, j, geoms[key]))
    return tiles, glist


NA_TILES, NA_GEOMS = na_geometry()
NG = len(NA_GEOMS)


def rope_tables():
    pos = np.arange(S, dtype=np.float32)
    inv = (1.0 / (np.float32(10000.0) ** (np.arange(0, 64, 2, dtype=np.float32) / np.float32(64)))).astype(np.float32)
    ang = (pos[:, None] * inv[None, :]).astype(np.float32)
    cos = np.cos(ang).astype(np.float32).T
    sin = np.sin(ang).astype(np.float32).T
    cs_ = np.concatenate([cos, cos, -sin, sin], 0)
    sc_ = np.concatenate([-sin, sin, cos, cos], 0)
    return np.ascontiguousarray(np.stack([cs_, sc_], 0))


def prep_shared(inp):
    w = {}
    f32 = lambda a: np.asarray(a, dtype=np.float32)
    w["gains"] = np.ascontiguousarray(np.concatenate([
        gain_cols(f32(inp["ffn1_norm"])[0]), gain_cols(f32(inp["mix_norm"])[0]),
        gain_cols(f32(inp["ffn2_norm"])[0]), gain_cols(f32(inp["pl_norm"])[0]),
        gain_cols(f32(inp["final_norm"])), gain_cols(f32(inp["q_a_norm"])[0]),
        gain_cols(f32(inp["kv_a_norm"])[0])], axis=1))
    for i in (1, 2):
        w["wg%d" % i] = lay(f32(inp["ffn%d_w_gate" % i])[0])
        w["wu%d" % i] = lay(f32(inp["ffn%d_w_up" % i])[0])
        w["wd%d" % i] = lay(f32(inp["ffn%d_w_down" % i])[0])
    win = f32(inp["w_in"])[0]
    w["w_qk"] = lay(win[:, 0:2048])
    w["w_vna"] = lay(win[:, 2048:3072], 512)
    w["w_lat"] = lay(win[:, 3072:4096])
    kr = win[:, 4096:4160]
    w["w_kr"] = lay(np.concatenate([kr, rot64(kr), rot64(kr), kr], axis=1))
    w["w_ga"] = lay(win[:, 4160:6208])
    w["w_gb"] = lay(win[:, 6208:8256])
    wuq = f32(inp["w_uq"])[0].reshape(512, NH, 192)
    w["w_uqn"] = lay(np.ascontiguousarray(wuq[:, :, :128]).reshape(512, NH * 128))
    qr = np.ascontiguousarray(wuq[:, :, 128:]).reshape(512, NH * 64)
    qrr = rot64(qr)
    w["w_uqr"] = lay(np.concatenate([qr.reshape(512, NH, 64), qrr.reshape(512, NH, 64)], axis=2).reshape(512, NH * 128))
    wukv = f32(inp["w_ukv"])[0].reshape(512, NH, 256)
    w["w_ukn"] = lay(np.ascontiguousarray(wukv[:, :, :128]).reshape(512, NH * 128))
    w["w_uv"] = lay(np.ascontiguousarray(wukv[:, :, 128:]).reshape(512, NH * 128), 512)
    w["w_a"] = lay(f32(inp["w_branch_a"])[0])
    w["w_b"] = lay(f32(inp["w_branch_b"])[0])
    w["w_out"] = lay(f32(inp["w_out"])[0])
    w["w_plg"] = lay(f32(inp["w_pl_gate"])[0])
    w["w_pl"] = lay(f32(inp["w_pl"])[0])
    w["ropec"] = rope_tables()
    rpb = f32(inp["na_rpb"])[0]
    dri = np.stack([g[0] for g in NA_GEOMS])
    dci = np.stack([g[1] for g in NA_GEOMS])
    w["nag"] = np.ascontiguousarray(rpb[:, dri, dci])
    w["nam"] = np.ascontiguousarray(np.stack([g[2] for g in NA_GEOMS]).astype(np.float32))
    return w


class Arena:
    def __init__(self, ap, ncols):
        self.ap = ap
        self.n = ncols
        self.off = 0

    def reset(self):
        self.off = 0

    def take(self, ncols, parts=128):
        assert self.off + ncols <= self.n, ("arena overflow", self.off, ncols, self.n)
        b = Buf(self.ap[0:parts, self.off:self.off + ncols])
        self.off += ncols
        return b


class KB:
    def __init__(self, nc, shared_shapes, phases, debug):
        self.nc = nc
        self.P = Prog(nc)
        self.keymap = {}
        self.phases = phases
        self.debug = debug
        self.t = {}
        dt = nc.dram_tensor
        self.t["xT"] = dt("xT", [D, S], F32, kind="ExternalInput").ap()
        self.t["pT"] = dt("pT", [256, S], F32, kind="ExternalInput").ap()
        for k, shp in shared_shapes.items():
            self.t[k] = dt(k, list(shp), F32, kind="ExternalInput").ap()
        self.t["outT"] = dt("outT", [D, S], F32, kind="ExternalOutput").ap()
        sk = "ExternalOutput" if debug else "Internal"
        for k in ("h1T", "h2T", "h3T"):
            self.t[k] = dt(k, [D, S], F32, kind=sk).ap()
        for k in ("qna", "kna", "qmn", "kmn", "oa", "ob"):
            self.t[k] = dt(k, [NH, 128, S], BF16, kind=sk).ap()
        self.t["qmr"] = dt("qmr", [NH, 128, S], BF16, kind=sk).ap()
        self.t["kmr"] = dt("kmr", [128, S], BF16, kind=sk).ap()
        self.t["vna"] = dt("vna", [S, NH * 128], BF16, kind=sk).ap()
        self.t["vm"] = dt("vm", [S, NH * 128], BF16, kind=sk).ap()

    def dma(self, key, out, in_, reads=(), writes=()):
        km = self.keymap
        if key not in km:
            km[key] = "k%d" % len(km)
        return self.P.dma("sp", km[key], lambda e: e.dma_start(out=out, in_=in_), reads=reads, writes=writes)

    def mm(self, ps, out, lhsT, rhs, start, stop, reads):
        return self.P.op("pe", lambda e: e.matmul(out, lhsT=lhsT, rhs=rhs, start=start, stop=stop),
                         reads=reads, writes=[ps])

    def act(self, out, in_, func, reads, writes, scale=None, bias=None):
        kw = {}
        if scale is not None:
            kw["scale"] = scale
        if bias is not None:
            kw["bias"] = bias
        return self.P.op("act", lambda e: e.activation(out=out, in_=in_, func=func, **kw), reads=reads, writes=writes)

    def cast(self, eng, out, in_, reads, writes):
        if eng == "act":
            return self.P.op("act", lambda e: e.activation(out=out, in_=in_, func=AF.Copy), reads=reads, writes=writes)
        return self.P.op(eng, lambda e: e.tensor_copy(out=out, in_=in_), reads=reads, writes=writes)

    def tt(self, eng, out, in0, in1, op, reads, writes):
        return self.P.op(eng, lambda e: e.tensor_tensor(out=out, in0=in0, in1=in1, op=op), reads=reads, writes=writes)

    def stt(self, eng, out, in0, scalar, in1, op0, op1, reads, writes):
        return self.P.op(eng, lambda e: e.scalar_tensor_tensor(out=out, in0=in0, scalar=scalar, in1=in1, op0=op0, op1=op1),
                         reads=reads, writes=writes)

    def wl_init(self, nstage=3, nwbf=4, cols=2048, cast_engs=("pool", "act")):
        self.wl_stage = [self.fa.take(cols) for _ in range(nstage)]
        self.wl_wbf = [self.ba.take(cols) for _ in range(nwbf)]
        self.wl_i = 0
        self.wl_j = 0
        self.wl_c = 0
        self.wl_engs = cast_engs

    def wl_load(self, src, n):
        si = self.wl_i % len(self.wl_stage)
        s = self.wl_stage[si]
        self.wl_i += 1
        b = self.wl_wbf[self.wl_j % len(self.wl_wbf)]
        self.wl_j += 1
        self.dma("st%d" % si, s.ap[:, :n], src, writes=[s])
        eng = self.wl_engs[self.wl_c % len(self.wl_engs)]
        self.wl_c += 1
        self.cast(eng, b.ap[:, :n], s.ap[:, :n], reads=[s], writes=[b])
        return b

    def linear_jobs(self, jobs, banks, NT, prefetch=2, ncols=512):
        seq = [(ji, b) for ji, j in enumerate(jobs) for b in range(j["nblk"])]
        loaded = {}
        st = {"nxt": 0}

        def ensure(i):
            while st["nxt"] <= min(i, len(seq) - 1):
                ji, b = seq[st["nxt"]]
                j = jobs[ji]
                loaded[st["nxt"]] = self.wl_load(j["wsrc"](b), (j["nk"] // j["nblk"]) * 128)
                st["nxt"] += 1

        for i, (ji, b) in enumerate(seq):
            ensure(i + prefetch)
            wb = loaded.pop(i)
            j = jobs[ji]
            subs = j.get("subs", ((0, 128),))
            nsub = len(subs)
            nk = j["nk"]
            kb = nk // j["nblk"]
            assert len(banks) >= 2 * NT * nsub
            for nt in range(NT):
                for si, (off, M) in enumerate(subs):
                    ps = banks[((ji % 2) * NT + nt) * nsub + si]
                    for k in range(kb):
                        kc = b * kb + k
                        rhs_ap, rhs_buf = j["rhs"](kc, nt)
                        self.mm(ps, ps.ap[0:M, 0:ncols], wb.ap[:, k * 128 + off:k * 128 + off + M], rhs_ap,
                                kc == 0, kc == nk - 1, [wb, rhs_buf])
                    if b == j["nblk"] - 1:
                        j["evac"](si, nt, ps)

    def linear_fm(self, wsrc, nk, nblk, n_out, rhs_fn, evac, banks, NT, subs=((0, 128),), prefetch=2):
        jobs = [dict(wsrc=(lambda b, n=n: wsrc(n, b)), nk=nk, nblk=nblk, rhs=rhs_fn,
                     evac=(lambda si, nt, ps, n=n: evac(n, si, nt, ps)), subs=subs) for n in range(n_out)]
        self.linear_jobs(jobs, banks, NT, prefetch)

    def linear_tm(self, wsrc, nk, n_cb, x_fn, ntc, evac, banks, kb=4):
        nblk = nk // kb
        for cb in range(n_cb):
            for b in range(nblk):
                wb = self.wl_load(wsrc(cb, b), kb * 512)
                for tc in range(ntc):
                    ps = banks[(cb % 2) * ntc + tc]
                    for k in range(kb):
                        kc = b * kb + k
                        x_ap, x_buf = x_fn(kc, tc)
                        self.mm(ps, ps.ap[:, :], x_ap, wb.ap[:, k * 512:(k + 1) * 512], kc == 0, kc == nk - 1, [wb, x_buf])
                    if b == nblk - 1:
                        evac(cb, tc, ps)

    def rmsnorm_fm(self, src_fn, nch, gcol, xn, Tn, Dn, xs_slots, sq_slots, rstd, banks):
        NT = Tn // 512

        def fetch(c, tag):
            s = src_fn(c)
            if s[0] == "sb":
                return s[1], s[2]
            xs = xs_slots[c % len(xs_slots)]
            self.dma("xs%d" % (c % len(xs_slots)), xs.ap[:, :Tn], s[1], writes=[xs])
            return xs.ap[:, :Tn], xs

        for c in range(nch):
            x_ap, x_buf = fetch(c, 0)
            sq = sq_slots[c % len(sq_slots)]
            self.act(sq.ap[:, :Tn], x_ap, AF.Square, reads=[x_buf], writes=[sq])
            for nt in range(NT):
                self.mm(banks[nt], banks[nt].ap[:, :], self.ones.ap[:, :], sq.ap[:, nt * 512:(nt + 1) * 512],
                        c == 0, c == nch - 1, [sq, self.ones])
        for nt in range(NT):
            self.act(rstd.ap[:, nt * 512:(nt + 1) * 512], banks[nt].ap[:, :], AF.Sqrt, reads=[banks[nt], self.epsb],
                     writes=[rstd], scale=1.0 / Dn, bias=self.epsb.ap[:, 0:1])
        self.P.op("dve", lambda e: e.reciprocal(out=rstd.ap[:, :Tn], in_=rstd.ap[:, :Tn]), reads=[rstd], writes=[rstd])
        for c in range(nch):
            x_ap, x_buf = fetch(c, 1)
            self.stt("dve", xn.ap[:, c * Tn:(c + 1) * Tn], x_ap, self.gains.ap[:, gcol + c:gcol + c + 1], rstd.ap[:, :Tn],
                     ALU.mult, ALU.mult, reads=[x_buf, rstd, self.gains], writes=[xn])

    def phase_begin(self):
        self.ba.reset()
        self.fa.reset()
        self.keymap = {}

    def phase_end(self):
        self.P.barrier()

    def ffn(self, src, dst, wg, wu, wd, gcol):
        T = 1024
        NT = T // 512
        t = self.t
        self.phase_begin()
        xn = self.ba.take(16 * T)
        A = [[self.ba.take(512) for _ in range(NT)] for _ in range(44)]
        sq_slots = [self.ba.take(T) for _ in range(2)]
        xs_slots = [self.fa.take(T) for _ in range(2)]
        rstd = self.fa.take(T)
        sg = [self.fa.take(512) for _ in range(NT)]
        xres = [self.fa.take(512) for _ in range(2)]
        hout = [self.fa.take(512) for _ in range(2)]
        self.wl_init()
        ps = self.ps
        for tt_ in range(S // T):
            t0 = tt_ * T
            self.rmsnorm_fm(lambda c: ("dram", t[src][c * 128:(c + 1) * 128, t0:t0 + T]), 16, gcol, xn, T, D,
                            xs_slots, sq_slots, rstd, ps[4:4 + NT])

            def rhs_x(kc, nt):
                return xn.ap[:, kc * T + nt * 512:kc * T + (nt + 1) * 512], xn

            def evac_gu(n, si, nt, pb):
                f = n // 2
                if n % 2 == 0:
                    self.act(sg[nt].ap[:, :], pb.ap[:, :], AF.Silu, reads=[pb], writes=[sg[nt]])
                else:
                    self.tt("dve", A[f][nt].ap[:, :], sg[nt].ap[:, :], pb.ap[:, :], ALU.mult, reads=[sg[nt], pb], writes=[A[f][nt]])

            self.linear_fm(lambda n, b: (t[wg] if n % 2 == 0 else t[wu])[n // 2].rearrange("p k j -> p (k j)"),
                           16, 1, 88, rhs_x, evac_gu, ps[0:4], NT)

            def rhs_a(kc, nt):
                return A[kc][nt].ap[:, :], A[kc][nt]

            cnt = {"i": 0}

            def evac_d(n, si, nt, pb):
                i = cnt["i"]
                cnt["i"] += 1
                xr = xres[i % 2]
                ho = hout[i % 2]
                self.dma("xr%d" % (i % 2), xr.ap[:, :], t[src][n * 128:(n + 1) * 128, t0 + nt * 512:t0 + (nt + 1) * 512], writes=[xr])
                self.stt("dve", ho.ap[:, :], pb.ap[:, :], 0.5, xr.ap[:, :], ALU.mult, ALU.add, reads=[pb, xr], writes=[ho])
                self.dma("ho%d" % (i % 2), t[dst][n * 128:(n + 1) * 128, t0 + nt * 512:t0 + (nt + 1) * 512], ho.ap[:, :], reads=[ho])

            self.linear_fm(lambda n, b: t[wd][n][:, b * 11:(b + 1) * 11, :].rearrange("p k j -> p (k j)"),
                           44, 4, 16, rhs_a, evac_d, ps[4:8], NT)
        self.phase_end()

    def evac_copy(self, i, out, in_, reads, writes):
        eng = "act" if i % 2 == 0 else "dve"
        return self.cast(eng, out, in_, reads, writes)

    def proj(self):
        T = 512
        t = self.t
        self.phase_begin()
        u = self.ba.take(16 * T)
        cq = self.ba.take(4 * T)
        ckv = self.ba.take(4 * T)
        sq_slots = [self.ba.take(T) for _ in range(2)]
        osb = [self.ba.take(512) for _ in range(3)]
        xs_slots = [self.fa.take(T) for _ in range(2)]
        rstd = self.fa.take(T)
        ql = self.fa.take(4 * T)
        kvl = self.fa.take(4 * T)
        Ct = self.fa.take(512)
        St = self.fa.take(512)
        t1 = self.fa.take(512)
        t2 = self.fa.take(512)
        self.wl_init()
        ps = self.ps
        cnt = {"o": 0}

        def out_bf(pb, M, dst):
            i = cnt["o"]
            cnt["o"] += 1
            o = osb[i % 3]
            self.evac_copy(i, o.ap[0:M, :], pb.ap[0:M, :], reads=[pb], writes=[o])
            self.dma("ob%d" % (i % 3), dst, o.ap[0:M, :], reads=[o])

        for tt_ in range(S // T):
            t0 = tt_ * T
            cs = slice(t0, t0 + T)
            self.rmsnorm_fm(lambda c: ("dram", t["h1T"][c * 128:(c + 1) * 128, cs]), 16, 16, u, T, D,
                            xs_slots, sq_slots, rstd, ps[4:5])
            self.dma("ropeC", Ct.ap[:, :], t["ropec"][0, :, cs], writes=[Ct])
            self.dma("ropeS", St.ap[:, :], t["ropec"][1, :, cs], writes=[St])

            def rhs_u(kc, nt):
                return u.ap[:, kc * T:(kc + 1) * T], u

            def rhs_cq(kc, nt):
                return cq.ap[:, kc * T:(kc + 1) * T], cq

            def rhs_ckv(kc, nt):
                return ckv.ap[:, kc * T:(kc + 1) * T], ckv

            def ev_kr(n, si, nt, pb):
                if n == 0:
                    self.tt("dve", t1.ap[:, :], pb.ap[:, :], Ct.ap[:, :], ALU.mult, reads=[pb, Ct], writes=[t1])
                else:
                    self.tt("dve", t2.ap[:, :], pb.ap[:, :], St.ap[:, :], ALU.mult, reads=[pb, St], writes=[t2])
                    i = cnt["o"]
                    cnt["o"] += 1
                    o = osb[i % 3]
                    self.tt("dve", o.ap[:, :], t1.ap[:, :], t2.ap[:, :], ALU.add, reads=[t1, t2], writes=[o])
                    self.dma("ob%d" % (i % 3), t["kmr"][:, cs], o.ap[:, :], reads=[o])

            def ev_qr(n, si, nt, pb):
                i = cnt["o"]
                cnt["o"] += 1
                o = osb[i % 3]
                self.tt("dve", o.ap[:, :], pb.ap[:, :], Ct.ap[:, :], ALU.mult, reads=[pb, Ct], writes=[o])
                self.dma("ob%d" % (i % 3), t["qmr"][n][:, cs], o.ap[:, :], reads=[o])

            import os
            PS_ = os.environ.get("PROJ_STEPS", "qk,lat,kr,vna,qn,qr,kn,vm").split(",")
            if "qk" in PS_:
              self.linear_fm(lambda n, b: t["w_qk"][n].rearrange("p k j -> p (k j)"), 16, 1, 16, rhs_u,
                           lambda n, si, nt, pb: out_bf(pb, 128, (t["qna"][n] if n < 8 else t["kna"][n - 8])[:, cs]),
                           ps[0:2], 1)
            def ev_lat(n, si, nt, pb):
                dstb = ql if n < 4 else kvl
                c = n % 4
                self.evac_copy(n, dstb.ap[:, c * T:(c + 1) * T], pb.ap[:, :], reads=[pb], writes=[dstb])
            if "lat" in PS_:
              self.linear_fm(lambda n, b: t["w_lat"][n].rearrange("p k j -> p (k j)"), 16, 1, 8, rhs_u, ev_lat, ps[0:2], 1)
            if "kr" in PS_:
              self.linear_fm(lambda n, b: t["w_kr"][n].rearrange("p k j -> p (k j)"), 16, 1, 2, rhs_u, ev_kr, ps[0:2], 1)
            def ev_vna(cb, tc, pb):
                out_bf(pb, 128, t["vna"][t0 + tc * 128:t0 + (tc + 1) * 128, cb * 512:(cb + 1) * 512])
            if "vna" in PS_:
              self.linear_tm(lambda cb, b: t["w_vna"][cb][:, b * 4:(b + 1) * 4, :].rearrange("p k j -> p (k j)"), 16, 2,
                           lambda kc, tc: (u.ap[:, kc * T + tc * 128:kc * T + (tc + 1) * 128], u), 4, ev_vna, ps[0:8])
            if "qn" in PS_ or "qr" in PS_:
              self.rmsnorm_fm(lambda c: ("sb", ql.ap[:, c * T:(c + 1) * T], ql), 4, 80, cq, T, 512,
                            xs_slots, sq_slots, rstd, ps[4:5])
            if "qn" in PS_:
              self.linear_fm(lambda n, b: t["w_uqn"][n].rearrange("p k j -> p (k j)"), 4, 1, 8, rhs_cq,
                           lambda n, si, nt, pb: out_bf(pb, 128, t["qmn"][n][:, cs]), ps[0:2], 1)
            if "qr" in PS_:
              self.linear_fm(lambda n, b: t["w_uqr"][n].rearrange("p k j -> p (k j)"), 4, 1, 8, rhs_cq, ev_qr, ps[0:2], 1)
            if "kn" in PS_ or "vm" in PS_:
              self.rmsnorm_fm(lambda c: ("sb", kvl.ap[:, c * T:(c + 1) * T], kvl), 4, 84, ckv, T, 512,
                            xs_slots, sq_slots, rstd, ps[4:5])
            if "kn" in PS_:
              self.linear_fm(lambda n, b: t["w_ukn"][n].rearrange("p k j -> p (k j)"), 4, 1, 8, rhs_ckv,
                           lambda n, si, nt, pb: out_bf(pb, 128, t["kmn"][n][:, cs]), ps[0:2], 1)
            def ev_vm(cb, tc, pb):
                out_bf(pb, 128, t["vm"][t0 + tc * 128:t0 + (tc + 1) * 128, cb * 512:(cb + 1) * 512])
            if "vm" in PS_:
              self.linear_tm(lambda cb, b: t["w_uv"][cb].rearrange("p k j -> p (k j)"), 4, 2,
                           lambda kc, tc: (ckv.ap[:, kc * T + tc * 128:kc * T + (tc + 1) * 128], ckv), 4, ev_vm, ps[0:8])
        self.phase_end()

    def attn_core(self, steps, load_head, S_mm, post_exp, scale, dst):
        ps = self.ps
        pexp = self.at_pexp
        nsteps = len(steps)

        def emit_S(i):
            h, qt, kc, first, last, extra = steps[i]
            S_mm(ps[i % 2], h, qt, kc)

        load_head(steps[0][0])
        emit_S(0)
        for i in range(nsteps):
            h, qt, kc, first, last, extra = steps[i]
            if i + 1 < nsteps:
                emit_S(i + 1)
            pS = ps[i % 2]
            pe_ = pexp[i % len(pexp)]
            self.act(pe_.ap[:, :], pS.ap[:, :], AF.Exp, reads=[pS], writes=[pe_], scale=scale)
            pr = post_exp(i, pe_, h, extra)
            O = ps[2 + qt % 2]
            Dn = ps[4 + qt % 2]
            v = self.at_v[h % 2]
            self.mm(O, O.ap[:, :], v.ap[:, kc * 128:(kc + 1) * 128], pr.ap[:, :], first, last, [v, pr])
            self.mm(Dn, Dn.ap[:, :], self.ones.ap[:, :], pr.ap[:, :], first, last, [self.ones, pr])
            if (i == 0 or steps[i - 1][0] != h) and h + 1 < NH:
                load_head(h + 1)
            if last:
                rec = self.at_rec[qt % 2]
                o = self.at_o[qt % 2]
                self.P.op("dve", lambda e, rec=rec, Dn=Dn: e.reciprocal(out=rec.ap[:, :], in_=Dn.ap[:, :]), reads=[Dn], writes=[rec])
                self.tt("dve", o.ap[:, :], O.ap[:, :], rec.ap[:, :], ALU.mult, reads=[O, rec], writes=[o])
                self.dma("ao%d" % (qt % 2), self.t[dst][h][:, qt * 512:(qt + 1) * 512], o.ap[:, :], reads=[o])

    def mla(self):
        t = self.t
        self.phase_begin()
        qn = [self.ba.take(S) for _ in range(2)]
        qr = [self.ba.take(S) for _ in range(2)]
        kn = [self.ba.take(S) for _ in range(2)]
        self.at_v = [self.ba.take(S) for _ in range(2)]
        kr = self.ba.take(S)
        self.at_pexp = [self.ba.take(512) for _ in range(3)]
        self.at_o = [self.ba.take(512) for _ in range(2)]
        self.at_rec = [self.fa.take(512) for _ in range(2)]
        self.dma("kr", kr.ap[:, :], t["kmr"][:, :], writes=[kr])

        def load_head(h):
            s = h % 2
            self.dma("qn%d" % s, qn[s].ap[:, :], t["qmn"][h], writes=[qn[s]])
            self.dma("qr%d" % s, qr[s].ap[:, :], t["qmr"][h], writes=[qr[s]])
            self.dma("kn%d" % s, kn[s].ap[:, :], t["kmn"][h], writes=[kn[s]])
            self.dma("v%d" % s, self.at_v[s].ap[:, :].rearrange("p (c j) -> p c j", j=128),
                     t["vm"][:, h * 128:(h + 1) * 128].rearrange("(c p) j -> p c j", p=128), writes=[self.at_v[s]])

        def S_mm(pS, h, qt, kc):
            s = h % 2
            self.mm(pS, pS.ap[:, :], kn[s].ap[:, kc * 128:(kc + 1) * 128], qn[s].ap[:, qt * 512:(qt + 1) * 512], True, False, [kn[s], qn[s]])
            self.mm(pS, pS.ap[:, :], kr.ap[:, kc * 128:(kc + 1) * 128], qr[s].ap[:, qt * 512:(qt + 1) * 512], False, True, [kr, qr[s]])

        steps = [(h, qt, kc, kc == 0, kc == 15, None) for h in range(NH) for qt in range(4) for kc in range(16)]
        self.attn_core(steps, load_head, S_mm, lambda i, pe_, h, extra: pe_, 192 ** -0.5, "ob")
        self.phase_end()

    def na(self):
        t = self.t
        self.phase_begin()
        q = [self.ba.take(S) for _ in range(2)]
        k = [self.ba.take(S) for _ in range(2)]
        self.at_v = [self.ba.take(S) for _ in range(2)]
        E = [[self.ba.take(512) for _ in range(NG)] for _ in range(2)]
        Mb = [self.ba.take(512) for _ in range(NG)]
        self.at_pexp = [self.ba.take(512) for _ in range(3)]
        pm = [self.ba.take(512) for _ in range(3)]
        self.at_o = [self.ba.take(512) for _ in range(2)]
        self.at_rec = [self.fa.take(512) for _ in range(2)]
        est = [self.fa.take(512) for _ in range(3)]
        etmp = [self.fa.take(512) for _ in range(2)]
        cnt = {"e": 0}
        for g in range(NG):
            i = cnt["e"]
            cnt["e"] += 1
            self.dma("es%d" % (i % 3), est[i % 3].ap[:, :], t["nam"][g], writes=[est[i % 3]])
            self.cast("pool", Mb[g].ap[:, :], est[i % 3].ap[:, :], reads=[est[i % 3]], writes=[Mb[g]])

        def load_head(h):
            s = h % 2
            self.dma("qn%d" % s, q[s].ap[:, :], t["qna"][h], writes=[q[s]])
            self.dma("kn%d" % s, k[s].ap[:, :], t["kna"][h], writes=[k[s]])
            self.dma("v%d" % s, self.at_v[s].ap[:, :].rearrange("p (c j) -> p c j", j=128),
                     t["vna"][:, h * 128:(h + 1) * 128].rearrange("(c p) j -> p c j", p=128), writes=[self.at_v[s]])
            for g in range(NG):
                i = cnt["e"]
                cnt["e"] += 1
                es = est[i % 3]
                et = etmp[i % 2]
                self.dma("es%d" % (i % 3), es.ap[:, :], t["nag"][h, g], writes=[es])
                self.act(et.ap[:, :], es.ap[:, :], AF.Exp, reads=[es], writes=[et])
                self.tt("pool", E[s][g].ap[:, :], et.ap[:, :], Mb[g].ap[:, :], ALU.mult, reads=[et, Mb[g]], writes=[E[s][g]])

        def S_mm(pS, h, qt, j):
            s = h % 2
            self.mm(pS, pS.ap[:, :], k[s].ap[:, j * 128:(j + 1) * 128], q[s].ap[:, qt * 512:(qt + 1) * 512], True, True, [k[s], q[s]])

        def post_exp(i, pe_, h, g):
            o = pm[i % 3]
            eng = "dve" if i % 2 == 0 else "pool"
            self.tt(eng, o.ap[:, :], pe_.ap[:, :], E[h % 2][g].ap[:, :], ALU.mult, reads=[pe_, E[h % 2][g]], writes=[o])
            return o

        steps = []
        for h in range(NH):
            for qt in range(4):
                tl = [(j, g) for (q_, j, g) in NA_TILES if q_ == qt]
                for ii, (j, g) in enumerate(tl):
                    steps.append((h, qt, j, ii == 0, ii == len(tl) - 1, g))
        self.attn_core(steps, load_head, S_mm, post_exp, 128 ** -0.5, "oa")
        self.phase_end()

    def merge(self):
        T = 1024
        NT = T // 512
        t = self.t
        self.phase_begin()
        u = self.ba.take(16 * T)
        oa = self.ba.take(8 * T)
        ob = self.ba.take(8 * T)
        mg = [[self.ba.take(512) for _ in range(NT)] for _ in range(16)]
        sq_slots = [self.ba.take(T) for _ in range(2)]
        xs_slots = [self.fa.take(T) for _ in range(2)]
        rstd = self.fa.take(T)
        sga = [self.fa.take(512) for _ in range(NT)]
        m1 = [self.fa.take(512) for _ in range(NT)]
        m2 = [self.fa.take(512) for _ in range(NT)]
        xres = [self.fa.take(512) for _ in range(2)]
        hout = [self.fa.take(512) for _ in range(2)]
        self.wl_init()
        ps = self.ps
        for tt_ in range(S // T):
            t0 = tt_ * T
            cs = slice(t0, t0 + T)
            self.rmsnorm_fm(lambda c: ("dram", t["h1T"][c * 128:(c + 1) * 128, cs]), 16, 16, u, T, D,
                            xs_slots, sq_slots, rstd, ps[4:4 + NT])
            for h in range(NH):
                self.dma("oa", oa.ap[:, h * T:(h + 1) * T], t["oa"][h][:, cs], writes=[oa])
            for h in range(NH):
                self.dma("ob", ob.ap[:, h * T:(h + 1) * T], t["ob"][h][:, cs], writes=[ob])
            rhs_u = lambda kc, nt: (u.ap[:, kc * T + nt * 512:kc * T + (nt + 1) * 512], u)
            rhs_oa = lambda kc, nt: (oa.ap[:, kc * T + nt * 512:kc * T + (nt + 1) * 512], oa)
            rhs_ob = lambda kc, nt: (ob.ap[:, kc * T + nt * 512:kc * T + (nt + 1) * 512], ob)
            jobs = []
            for n in range(16):
                def ev_g(si, nt, pb):
                    self.act(sga[nt].ap[:, :], pb.ap[:, :], AF.Sigmoid, reads=[pb], writes=[sga[nt]])

                def ev_ya(si, nt, pb):
                    self.tt("dve", m1[nt].ap[:, :], sga[nt].ap[:, :], pb.ap[:, :], ALU.mult, reads=[sga[nt], pb], writes=[m1[nt]])

                def ev_yb(si, nt, pb, n=n):
                    self.tt("dve", m2[nt].ap[:, :], sga[nt].ap[:, :], pb.ap[:, :], ALU.mult, reads=[sga[nt], pb], writes=[m2[nt]])
                    self.tt("pool", mg[n][nt].ap[:, :], m1[nt].ap[:, :], m2[nt].ap[:, :], ALU.add, reads=[m1[nt], m2[nt]], writes=[mg[n][nt]])

                jobs.append(dict(wsrc=lambda b, n=n: t["w_ga"][n].rearrange("p k j -> p (k j)"), nk=16, nblk=1, rhs=rhs_u, evac=ev_g))
                jobs.append(dict(wsrc=lambda b, n=n: t["w_a"][n].rearrange("p k j -> p (k j)"), nk=8, nblk=1, rhs=rhs_oa, evac=ev_ya))
                jobs.append(dict(wsrc=lambda b, n=n: t["w_gb"][n].rearrange("p k j -> p (k j)"), nk=16, nblk=1, rhs=rhs_u, evac=ev_g))
                jobs.append(dict(wsrc=lambda b, n=n: t["w_b"][n].rearrange("p k j -> p (k j)"), nk=8, nblk=1, rhs=rhs_ob, evac=ev_yb))
            self.linear_jobs(jobs, ps[0:4], NT)
            cnt = {"i": 0}

            def evac_o(n, si, nt, pb):
                i = cnt["i"]
                cnt["i"] += 1
                xr = xres[i % 2]
                ho = hout[i % 2]
                self.dma("xr%d" % (i % 2), xr.ap[:, :], t["h1T"][n * 128:(n + 1) * 128, t0 + nt * 512:t0 + (nt + 1) * 512], writes=[xr])
                self.tt("dve", ho.ap[:, :], pb.ap[:, :], xr.ap[:, :], ALU.add, reads=[pb, xr], writes=[ho])
                self.dma("ho%d" % (i % 2), t["h2T"][n * 128:(n + 1) * 128, t0 + nt * 512:t0 + (nt + 1) * 512], ho.ap[:, :], reads=[ho])

            self.linear_fm(lambda n, b: t["w_out"][n].rearrange("p k j -> p (k j)"), 16, 1, 16,
                           lambda kc, nt: (mg[kc][nt].ap[:, :], mg[kc][nt]), evac_o, ps[4:8], NT)
        self.phase_end()

    def pl(self):
        T = 512
        t = self.t
        self.phase_begin()
        v = self.ba.take(16 * T)
        pb16 = self.ba.take(2 * T)
        sq_slots = [self.ba.take(T) for _ in range(2)]
        xs_slots = [self.fa.take(T) for _ in range(2)]
        rstd = self.fa.take(T)
        h4 = [self.fa.take(T) for _ in range(16)]
        sgl = self.fa.take(T)
        self.wl_init()
        ps = self.ps
        for tt_ in range(S // T):
            t0 = tt_ * T
            cs = slice(t0, t0 + T)
            self.rmsnorm_fm(lambda c: ("dram", t["h3T"][c * 128:(c + 1) * 128, cs]), 16, 48, v, T, D,
                            xs_slots, sq_slots, rstd, ps[4:5])
            for c in range(2):
                xs = xs_slots[c]
                self.dma("xs%d" % c, xs.ap[:, :], t["pT"][c * 128:(c + 1) * 128, cs], writes=[xs])
                self.cast("pool", pb16.ap[:, c * T:(c + 1) * T], xs.ap[:, :], reads=[xs], writes=[pb16])
            jobs = []
            for n in range(16):
                def ev_g(si, nt, pb):
                    self.act(sgl.ap[:, :], pb.ap[:, :], AF.Sigmoid, reads=[pb], writes=[sgl])

                def ev_e(si, nt, pb, n=n):
                    xs = xs_slots[n % 2]
                    self.dma("xs%d" % (n % 2), xs.ap[:, :], t["h3T"][n * 128:(n + 1) * 128, cs], writes=[xs])
                    self.tt("dve", sgl.ap[:, :], sgl.ap[:, :], pb.ap[:, :], ALU.mult, reads=[sgl, pb], writes=[sgl])
                    self.tt("dve", h4[n].ap[:, :], sgl.ap[:, :], xs.ap[:, :], ALU.add, reads=[sgl, xs], writes=[h4[n]])

                jobs.append(dict(wsrc=lambda b, n=n: t["w_plg"][n].rearrange("p k j -> p (k j)"), nk=16, nblk=1,
                                 rhs=lambda kc, nt: (v.ap[:, kc * T:(kc + 1) * T], v), evac=ev_g))
                jobs.append(dict(wsrc=lambda b, n=n: t["w_pl"][n].rearrange("p k j -> p (k j)"), nk=2, nblk=1,
                                 rhs=lambda kc, nt: (pb16.ap[:, kc * T:(kc + 1) * T], pb16), evac=ev_e))
            self.linear_jobs(jobs, ps[0:2], 1)
            for c in range(16):
                sq = sq_slots[c % 2]
                self.act(sq.ap[:, :], h4[c].ap[:, :], AF.Square, reads=[h4[c]], writes=[sq])
                self.mm(ps[5], ps[5].ap[:, :], self.ones.ap[:, :], sq.ap[:, :], c == 0, c == 15, [sq, self.ones])
            self.act(rstd.ap[:, :], ps[5].ap[:, :], AF.Sqrt, reads=[ps[5], self.epsb], writes=[rstd], scale=1.0 / D,
                     bias=self.epsb.ap[:, 0:1])
            self.P.op("dve", lambda e: e.reciprocal(out=rstd.ap[:, :], in_=rstd.ap[:, :]), reads=[rstd], writes=[rstd])
            for c in range(16):
                self.stt("dve", h4[c].ap[:, :], h4[c].ap[:, :], self.gains.ap[:, 64 + c:65 + c], rstd.ap[:, :],
                         ALU.mult, ALU.mult, reads=[h4[c], rstd, self.gains], writes=[h4[c]])
                self.dma("out%d" % (c % 2), t["outT"][c * 128:(c + 1) * 128, cs], h4[c].ap[:, :], reads=[h4[c]])
            self.P.barrier()
        self.phase_end()

    def build(self):
        nc = self.nc
        P = self.P
        t = self.t
        with contextlib.ExitStack() as st:
            BCOLS = 71680
            FCOLS = 16384
            ba_t = st.enter_context(nc.sbuf_tensor("bf_arena", [128, BCOLS], BF16))
            fa_t = st.enter_context(nc.sbuf_tensor("f32_arena", [128, FCOLS], F32))
            self.ba = Arena(ba_t, BCOLS)
            self.fa = Arena(fa_t, FCOLS)
            self.ones = Buf(st.enter_context(nc.sbuf_tensor("ones_sb", [128, 128], BF16)))
            self.epsb = Buf(st.enter_context(nc.sbuf_tensor("epsb", [128, 1], F32)))
            self.gains = Buf(st.enter_context(nc.sbuf_tensor("gains_sb", [128, 88], F32)))
            self.ps = [Buf(st.enter_context(nc.psum_tensor("ps%d" % i, [128, 512], F32)), excl=True) for i in range(8)]
            P.op("dve", lambda e: e.memset(self.ones.ap[:, :], 1.0), writes=[self.ones])
            P.op("dve", lambda e: e.memset(self.epsb.ap[:, :], EPS), writes=[self.epsb])
            self.dma("gains", self.gains.ap[:, :], t["gains"][:, :], writes=[self.gains])
            if "ffn1" in self.phases:
                self.ffn("xT", "h1T", "wg1", "wu1", "wd1", 0)
            if "proj" in self.phases:
                self.proj()
            if "mla" in self.phases:
                self.mla()
            if "na" in self.phases:
                self.na()
            if "merge" in self.phases:
                self.merge()
            if "ffn2" in self.phases:
                self.ffn("h2T", "h3T", "wg2", "wu2", "wd2", 32)
            if "pl" in self.phases:
                self.pl()
            P.finish()
            P.emit()


def build_program(shared_shapes, phases=ALL_PHASES, debug=False):
    nc = bass.Bass("TRN2", target_bir_lowering=False)
    kb = KB(nc, shared_shapes, phases, debug)
    kb.build()
    return nc, kb


def kernel(**inputs):
    x = np.asarray(inputs["x"], dtype=np.float32)
    p = np.asarray(inputs["p"], dtype=np.float32)
    shared = prep_shared(inputs)
    nc, kb = build_program({k: v.shape for k, v in shared.items()})
    in_maps = []
    for b in range(8):
        m = dict(shared)
        m["xT"] = np.ascontiguousarray(x[b].T)
        m["pT"] = np.ascontiguousarray(p[0, b].T)
        in_maps.append(m)
    res = run_bass_kernel_spmd(nc, in_maps, core_ids=list(range(8)))
    out = np.stack([np.ascontiguousarray(r["outT"].T) for r in res.results], axis=0)
    return out.astype(np.float32)
```

```python
import contextlib
import numpy as np
import concourse.bass as bass
import concourse.mybir as mybir
from concourse.bass_utils import run_bass_kernel_spmd

F32 = mybir.dt.float32
BF16 = mybir.dt.bfloat16
AF = mybir.ActivationFunctionType
ALU = mybir.AluOpType

D = 2048
S = 2048
DFF = 5632
NH = 8
EPS = 1e-6
QNAMES = ("pe", "act", "dve", "pool", "sp")
ALL_PHASES = ("ffn1", "proj", "mla", "na", "merge", "ffn2", "pl")


class Buf:
    __slots__ = ("ap", "w", "r", "excl")

    def __init__(self, ap=None, excl=False):
        self.ap = ap
        self.w = []
        self.r = {}
        self.excl = excl


class Prog:
    def __init__(self, nc):
        self.nc = nc
        self.q = {n: [] for n in QNAMES}
        self.dma_count = {}
        self.pending = {n: [] for n in QNAMES}

    @staticmethod
    def _split(reads, writes):
        ex = [b for b in reads if b.excl]
        if ex:
            reads = [b for b in reads if not b.excl]
            writes = list(writes) + ex
        return reads, writes

    def _deps(self, reads, writes):
        deps = []
        for b in reads:
            deps += b.w
        for b in writes:
            deps += b.w
            deps += list(b.r.values())
        return deps

    def _mark(self, ev, reads, writes):
        for b in reads:
            if ev[0] == "e":
                b.r[ev[1]] = ev
            else:
                b.r[("d", ev[1])] = ev
        for b in writes:
            b.w = [ev]
            b.r = {}

    def op(self, q, fn, reads=(), writes=(), extra=()):
        reads, writes = self._split(reads, writes)
        deps = self._deps(reads, writes) + list(extra) + self.pending[q]
        self.pending[q] = []
        ev = ("e", q, len(self.q[q]))
        self.q[q].append(dict(fn=fn, deps=deps, signal=False, dma=None))
        self._mark(ev, reads, writes)
        return ev

    def dma(self, q, semkey, fn, reads=(), writes=(), extra=()):
        reads, writes = self._split(reads, writes)
        deps = self._deps(reads, writes) + list(extra) + self.pending[q]
        self.pending[q] = []
        c = self.dma_count.get(semkey, 0) + 16
        self.dma_count[semkey] = c
        ev = ("d", semkey, c)
        self.q[q].append(dict(fn=fn, deps=deps, signal=False, dma=semkey))
        self._mark(ev, reads, writes)
        return ev

    def barrier(self):
        evs = []
        for n in QNAMES:
            if self.q[n] and n != "sp":
                evs.append(("e", n, len(self.q[n]) - 1))
        for k, c in self.dma_count.items():
            evs.append(("d", k, c))
        for n in QNAMES:
            self.pending[n] = self.pending[n] + evs

    def finish(self):
        self.barrier()
        for n in QNAMES:
            if n != "sp":
                self.pending[n] = []
        self.op("sp", lambda e: e.nop())

    def emit(self):
        nc = self.nc
        for n in QNAMES:
            seen_e = {m: -1 for m in QNAMES}
            seen_d = {}
            for i, o in enumerate(self.q[n]):
                best = {}
                for ev in o["deps"]:
                    if ev[0] == "e":
                        _, m, idx = ev
                        if m == n and n in ("pe", "sp"):
                            continue
                        if idx <= seen_e[m]:
                            continue
                    else:
                        if ev[2] <= seen_d.get(ev[1], 0):
                            continue
                    key = (ev[0], ev[1])
                    if key not in best or ev[2] > best[key][2]:
                        best[key] = ev
                o["kept"] = list(best.values())
                for ev in o["kept"]:
                    if ev[0] == "e":
                        seen_e[ev[1]] = ev[2]
                        self.q[ev[1]][ev[2]]["signal"] = True
                    else:
                        seen_d[ev[1]] = ev[2]
        val = {}
        for n in QNAMES:
            c = 0
            v = []
            for o in self.q[n]:
                if o["signal"]:
                    c += 1
                v.append(c)
            val[n] = v
        self.stats = {n: (len(self.q[n]), val[n][-1] if val[n] else 0,
                          sum(len(o["kept"]) for o in self.q[n])) for n in QNAMES}
        with contextlib.ExitStack() as st:
            esem = {n: st.enter_context(nc.semaphore("s_" + n)) for n in QNAMES}
            dsem = {k: st.enter_context(nc.semaphore("d_%s" % (k,))) for k in self.dma_count}
            block = st.enter_context(nc.Block())

            def run(n, eng):
                for o in self.q[n]:
                    for ev in o["kept"]:
                        if ev[0] == "e":
                            eng.wait_ge(esem[ev[1]], val[ev[1]][ev[2]])
                        else:
                            eng.wait_ge(dsem[ev[1]], ev[2])
                    inst = o["fn"](eng)
                    if o["dma"] is not None:
                        inst.then_inc(dsem[o["dma"]], 16)
                    elif o["signal"]:
                        inst.then_inc(esem[n], 1)

            @block.tensor
            def _(eng):
                run("pe", eng)

            @block.scalar
            def _(eng):
                run("act", eng)

            @block.vector
            def _(eng):
                run("dve", eng)

            @block.gpsimd
            def _(eng):
                run("pool", eng)

            @block.sync
            def _(eng):
                run("sp", eng)


def lay(W, nb=128):
    K, N = W.shape
    return np.ascontiguousarray(W.reshape(K // 128, 128, N // nb, nb).transpose(2, 1, 0, 3))


def gain_cols(g):
    return np.ascontiguousarray(g.reshape(-1, 128).T)


def rot64(W):
    K, N = W.shape
    return W.reshape(K, N // 64, 2, 32)[:, :, ::-1, :].reshape(K, N)


def na_geometry():
    rows, Wd, kh, kw = 32, 64, 8, 16
    r = np.arange(rows)
    rs = np.clip(r - kh // 2, 0, rows - kh)
    c = np.arange(Wd)
    cs = np.clip(c - kw // 2, 0, Wd - kw)
    tiles = []
    geoms = {}
    glist = []
    for qt in range(4):
        qr = qt * 8 + np.arange(8)
        for j in range(16):
            kr = 2 * j + np.arange(2)
            rowok = (kr[:, None] >= rs[qr][None, :]) & (kr[:, None] < rs[qr][None, :] + kh)
            if not rowok.any():
                continue
            dr = kr[:, None] - qr[None, :] + (8 - 1)
            colok = (c[:, None] >= cs[None, :]) & (c[:, None] < cs[None, :] + kw)
            dc = np.clip(c[:, None] - c[None, :], -(kw - 1), kw - 1) + (kw - 1)
            mask = (rowok[:, None, :, None] & colok[None, :, None, :]).reshape(128, 512)
            dri = np.broadcast_to(np.clip(dr, 0, 14)[:, None, :, None], (2, 64, 8, 64)).reshape(128, 512)
            dci = np.broadcast_to(dc[None, :, None, :], (2, 64, 8, 64)).reshape(128, 512)
            key = (mask.tobytes(), (dri * mask).tobytes())
            if key not in geoms:
                geoms[key] = len(glist)
                glist.append((dri.copy(), dci.copy(), mask.copy()))
            tiles.append((qt, j, geoms[key]))
    return tiles, glist


NA_TILES, NA_GEOMS = na_geometry()
NG = len(NA_GEOMS)


def rope_tables():
    pos = np.arange(S, dtype=np.float32)
    inv = (1.0 / (np.float32(10000.0) ** (np.arange(0, 64, 2, dtype=np.float32) / np.float32(64)))).astype(np.float32)
    ang = (pos[:, None] * inv[None, :]).astype(np.float32)
    cos = np.cos(ang).astype(np.float32).T
    sin = np.sin(ang).astype(np.float32).T
    cs_ = np.concatenate([cos, cos, -sin, sin], 0)
    sc_ = np.concatenate([-sin, sin, cos, cos], 0)
    return np.ascontiguousarray(np.stack([cs_, sc_], 0))


def prep_shared(inp):
    w = {}
    f32 = lambda a: np.asarray(a, dtype=np.float32)
    w["gains"] = np.ascontiguousarray(np.concatenate([
        gain_cols(f32(inp["ffn1_norm"])[0]), gain_cols(f32(inp["mix_norm"])[0]),
        gain_cols(f32(inp["ffn2_norm"])[0]), gain_cols(f32(inp["pl_norm"])[0]),
        gain_cols(f32(inp["final_norm"])), gain_cols(f32(inp["q_a_norm"])[0]),
        gain_cols(f32(inp["kv_a_norm"])[0])], axis=1))
    w["wg1"] = lay(f32(inp["ffn1_w_gate"])[0])
    w["wu1"] = lay(f32(inp["ffn1_w_up"])[0])
    w["wd1"] = lay(f32(inp["ffn1_w_down"])[0])
    w["wg2"] = lay(f32(inp["ffn2_w_gate"])[0])
    w["wu2"] = lay(f32(inp["ffn2_w_up"])[0])
    w["wd2"] = lay(f32(inp["ffn2_w_down"])[0])
    win = f32(inp["w_in"])[0]
    w["w_qk"] = lay(win[:, 0:2048])
    w["w_vna"] = lay(win[:, 2048:3072], 512)
    w["w_lat"] = lay(win[:, 3072:4096])
    kr = win[:, 4096:4160]
    w["w_kr"] = lay(np.concatenate([kr, rot64(kr), rot64(kr), kr], axis=1))
    w["w_ga"] = lay(win[:, 4160:6208])
    w["w_gb"] = lay(win[:, 6208:8256])
    wuq = f32(inp["w_uq"])[0].reshape(512, NH, 192)
    w["w_uqn"] = lay(np.ascontiguousarray(wuq[:, :, :128]).reshape(512, NH * 128))
    qr = np.ascontiguousarray(wuq[:, :, 128:]).reshape(512, NH * 64)
    qrr = rot64(qr)
    w["w_uqr"] = lay(np.concatenate([qr.reshape(512, NH, 64), qrr.reshape(512, NH, 64)], axis=2).reshape(512, NH * 128))
    wukv = f32(inp["w_ukv"])[0].reshape(512, NH, 256)
    w["w_ukn"] = lay(np.ascontiguousarray(wukv[:, :, :128]).reshape(512, NH * 128))
    w["w_uv"] = lay(np.ascontiguousarray(wukv[:, :, 128:]).reshape(512, NH * 128), 512)
    w["w_a"] = lay(f32(inp["w_branch_a"])[0])
    w["w_b"] = lay(f32(inp["w_branch_b"])[0])
    w["w_out"] = lay(f32(inp["w_out"])[0])
    w["w_plg"] = lay(f32(inp["w_pl_gate"])[0])
    w["w_pl"] = lay(f32(inp["w_pl"])[0])
    w["ropec"] = rope_tables()
    rpb = f32(inp["na_rpb"])[0]
    dri = np.stack([g[0] for g in NA_GEOMS])
    dci = np.stack([g[1] for g in NA_GEOMS])
    w["nag"] = np.ascontiguousarray(rpb[:, dri, dci])
    w["nam"] = np.ascontiguousarray(np.stack([g[2] for g in NA_GEOMS]).astype(np.float32))
    return w


class Arena:
    def __init__(self, ap, ncols):
        self.ap = ap
        self.n = ncols
        self.off = 0

    def reset(self):
        self.off = 0

    def take(self, ncols, parts=128):
        assert self.off + ncols <= self.n, ("arena overflow", self.off, ncols, self.n)
        b = Buf(self.ap[0:parts, self.off:self.off + ncols])
        self.off += ncols
        return b


class KB:
    def __init__(self, nc, shared_shapes, phases, debug):
        self.nc = nc
        self.P = Prog(nc)
        self.keymap = {}
        self.phases = phases
        self.debug = debug
        self.t = {}
        dt = nc.dram_tensor
        self.t["xT"] = dt("xT", [D, S], F32, kind="ExternalInput").ap()
        self.t["pT"] = dt("pT", [256, S], F32, kind="ExternalInput").ap()
        for k, shp in shared_shapes.items():
            self.t[k] = dt(k, list(shp), F32, kind="ExternalInput").ap()
        self.t["outT"] = dt("outT", [D, S], F32, kind="ExternalOutput").ap()
        sk = "ExternalOutput" if debug else "Internal"
        for k in ("h1T", "h2T", "h3T"):
            self.t[k] = dt(k, [D, S], F32, kind=sk).ap()
        for k in ("qna", "kna", "qmn", "kmn", "oa", "ob"):
            self.t[k] = dt(k, [NH, 128, S], BF16, kind=sk).ap()
        self.t["qmr"] = dt("qmr", [NH, 128, S], BF16, kind=sk).ap()
        self.t["kmr"] = dt("kmr", [128, S], BF16, kind=sk).ap()
        self.t["vna"] = dt("vna", [S, NH * 128], BF16, kind=sk).ap()
        self.t["vm"] = dt("vm", [S, NH * 128], BF16, kind=sk).ap()

    def dma(self, key, out, in_, reads=(), writes=()):
        km = self.keymap
        if key not in km:
            km[key] = "k%d" % len(km)
        return self.P.dma("sp", km[key], lambda e: e.dma_start(out=out, in_=in_), reads=reads, writes=writes)

    def mm(self, ps, out, lhsT, rhs, start, stop, reads):
        return self.P.op("pe", lambda e: e.matmul(out, lhsT=lhsT, rhs=rhs, start=start, stop=stop),
                         reads=reads, writes=[ps])

    def act(self, out, in_, func, reads, writes, scale=None, bias=None):
        kw = {}
        if scale is not None:
            kw["scale"] = scale
        if bias is not None:
            kw["bias"] = bias
        return self.P.op("act", lambda e: e.activation(out=out, in_=in_, func=func, **kw), reads=reads, writes=writes)

    def cast(self, eng, out, in_, reads, writes):
        if eng == "act":
            return self.P.op("act", lambda e: e.activation(out=out, in_=in_, func=AF.Copy), reads=reads, writes=writes)
        return self.P.op(eng, lambda e: e.tensor_copy(out=out, in_=in_), reads=reads, writes=writes)

    def tt(self, eng, out, in0, in1, op, reads, writes):
        return self.P.op(eng, lambda e: e.tensor_tensor(out=out, in0=in0, in1=in1, op=op), reads=reads, writes=writes)

    def stt(self, eng, out, in0, scalar, in1, op0, op1, reads, writes):
        return self.P.op(eng, lambda e: e.scalar_tensor_tensor(out=out, in0=in0, scalar=scalar, in1=in1, op0=op0, op1=op1),
                         reads=reads, writes=writes)

    def wl_init(self, nstage=3, nwbf=4, cols=2048, cast_engs=("pool", "act")):
        self.wl_stage = [self.fa.take(cols) for _ in range(nstage)]
        self.wl_wbf = [self.ba.take(cols) for _ in range(nwbf)]
        self.wl_i = 0
        self.wl_j = 0
        self.wl_c = 0
        self.wl_engs = cast_engs

    def wl_load(self, src, n):
        si = self.wl_i % len(self.wl_stage)
        s = self.wl_stage[si]
        self.wl_i += 1
        b = self.wl_wbf[self.wl_j % len(self.wl_wbf)]
        self.wl_j += 1
        self.dma("st%d" % si, s.ap[:, :n], src, writes=[s])
        eng = self.wl_engs[self.wl_c % len(self.wl_engs)]
        self.wl_c += 1
        self.cast(eng, b.ap[:, :n], s.ap[:, :n], reads=[s], writes=[b])
        return b

    def linear_jobs(self, jobs, banks, NT, prefetch=2, ncols=512):
        seq = [(ji, b) for ji, j in enumerate(jobs) for b in range(j["nblk"])]
        loaded = {}
        st = {"nxt": 0}

        def ensure(i):
            while st["nxt"] <= min(i, len(seq) - 1):
                ji, b = seq[st["nxt"]]
                j = jobs[ji]
                loaded[st["nxt"]] = self.wl_load(j["wsrc"](b), (j["nk"] // j["nblk"]) * 128)
                st["nxt"] += 1

        for i, (ji, b) in enumerate(seq):
            ensure(i + prefetch)
            wb = loaded.pop(i)
            j = jobs[ji]
            subs = j.get("subs", ((0, 128),))
            nsub = len(subs)
            nk = j["nk"]
            kb = nk // j["nblk"]
            assert len(banks) >= 2 * NT * nsub
            for nt in range(NT):
                for si, (off, M) in enumerate(subs):
                    ps = banks[((ji % 2) * NT + nt) * nsub + si]
                    for k in range(kb):
                        kc = b * kb + k
                        rhs_ap, rhs_buf = j["rhs"](kc, nt)
                        self.mm(ps, ps.ap[0:M, 0:ncols], wb.ap[:, k * 128 + off:k * 128 + off + M], rhs_ap,
                                kc == 0, kc == nk - 1, [wb, rhs_buf])
                    if b == j["nblk"] - 1:
                        j["evac"](si, nt, ps)

    def linear_fm(self, wsrc, nk, nblk, n_out, rhs_fn, evac, banks, NT, subs=((0, 128),), prefetch=2):
        jobs = [dict(wsrc=(lambda b, n=n: wsrc(n, b)), nk=nk, nblk=nblk, rhs=rhs_fn,
                     evac=(lambda si, nt, ps, n=n: evac(n, si, nt, ps)), subs=subs) for n in range(n_out)]
        self.linear_jobs(jobs, banks, NT, prefetch)

    def linear_tm(self, wsrc, nk, n_cb, x_fn, ntc, evac, banks, kb=4):
        nblk = nk // kb
        seq = [(cb, b) for cb in range(n_cb) for b in range(nblk)]
        loaded = {0: self.wl_load(wsrc(*seq[0]), kb * 512)}
        for i, (cb, b) in enumerate(seq):
            if i + 1 < len(seq):
                loaded[i + 1] = self.wl_load(wsrc(*seq[i + 1]), kb * 512)
            wb = loaded.pop(i)
            for tc in range(ntc):
                ps = banks[(cb % 2) * ntc + tc]
                for k in range(kb):
                    kc = b * kb + k
                    x_ap, x_buf = x_fn(kc, tc)
                    self.mm(ps, ps.ap[:, :], x_ap, wb.ap[:, k * 512:(k + 1) * 512], kc == 0, kc == nk - 1, [wb, x_buf])
                if b == nblk - 1:
                    evac(cb, tc, ps)

    def rmsnorm_fm(self, src_fn, nch, gcol, xn, Tn, Dn, xs_slots, sq_slots, rstd, banks):
        NT = Tn // 512

        def fetch(c, tag):
            s = src_fn(c)
            if s[0] == "sb":
                return s[1], s[2]
            xs = xs_slots[c % len(xs_slots)]
            self.dma("xs%d" % (c % len(xs_slots)), xs.ap[:, :Tn], s[1], writes=[xs])
            return xs.ap[:, :Tn], xs

        for c in range(nch):
            x_ap, x_buf = fetch(c, 0)
            sq = sq_slots[c % len(sq_slots)]
            self.act(sq.ap[:, :Tn], x_ap, AF.Square, reads=[x_buf], writes=[sq])
            for nt in range(NT):
                self.mm(banks[nt], banks[nt].ap[:, :], self.ones.ap[:, :], sq.ap[:, nt * 512:(nt + 1) * 512],
                        c == 0, c == nch - 1, [sq, self.ones])
        for nt in range(NT):
            self.act(rstd.ap[:, nt * 512:(nt + 1) * 512], banks[nt].ap[:, :], AF.Sqrt, reads=[banks[nt], self.epsb],
                     writes=[rstd], scale=1.0 / Dn, bias=self.epsb.ap[:, 0:1])
        self.P.op("dve", lambda e: e.reciprocal(out=rstd.ap[:, :Tn], in_=rstd.ap[:, :Tn]), reads=[rstd], writes=[rstd])
        for c in range(nch):
            x_ap, x_buf = fetch(c, 1)
            self.stt("dve", xn.ap[:, c * Tn:(c + 1) * Tn], x_ap, self.gains.ap[:, gcol + c:gcol + c + 1], rstd.ap[:, :Tn],
                     ALU.mult, ALU.mult, reads=[x_buf, rstd, self.gains], writes=[xn])

    def phase_begin(self):
        self.ba.reset()
        self.fa.reset()
        self.keymap = {}

    def phase_end(self):
        self.P.barrier()

    def ffn(self, src, dst, wg, wu, wd, gcol):
        T = 1024
        NT = T // 512
        t = self.t
        self.phase_begin()
        xn = self.ba.take(16 * T)
        A = [[self.ba.take(512) for _ in range(NT)] for _ in range(44)]
        sq_slots = [self.ba.take(T) for _ in range(2)]
        xs_slots = [self.fa.take(T) for _ in range(2)]
        rstd = self.fa.take(T)
        sg = [self.fa.take(512) for _ in range(NT)]
        xres = [self.fa.take(512) for _ in range(2)]
        hout = [self.fa.take(512) for _ in range(2)]
        self.wl_init()
        ps = self.ps
        for tt_ in range(S // T):
            t0 = tt_ * T
            self.rmsnorm_fm(lambda c: ("dram", t[src][c * 128:(c + 1) * 128, t0:t0 + T]), 16, gcol, xn, T, D,
                            xs_slots, sq_slots, rstd, ps[4:4 + NT])

            def rhs_x(kc, nt):
                return xn.ap[:, kc * T + nt * 512:kc * T + (nt + 1) * 512], xn

            def evac_gu(n, si, nt, pb):
                f = n // 2
                if n % 2 == 0:
                    self.act(sg[nt].ap[:, :], pb.ap[:, :], AF.Silu, reads=[pb], writes=[sg[nt]])
                else:
                    self.tt("dve", A[f][nt].ap[:, :], sg[nt].ap[:, :], pb.ap[:, :], ALU.mult, reads=[sg[nt], pb], writes=[A[f][nt]])

            self.linear_fm(lambda n, b: (t[wg] if n % 2 == 0 else t[wu])[n // 2].rearrange("p k j -> p (k j)"),
                           16, 1, 88, rhs_x, evac_gu, ps[0:4], NT)

            def rhs_a(kc, nt):
                return A[kc][nt].ap[:, :], A[kc][nt]

            cnt = {"i": 0}

            def evac_d(n, si, nt, pb):
                i = cnt["i"]
                cnt["i"] += 1
                xr = xres[i % 2]
                ho = hout[i % 2]
                self.dma("xr%d" % (i % 2), xr.ap[:, :], t[src][n * 128:(n + 1) * 128, t0 + nt * 512:t0 + (nt + 1) * 512], writes=[xr])
                self.stt("dve", ho.ap[:, :], pb.ap[:, :], 0.5, xr.ap[:, :], ALU.mult, ALU.add, reads=[pb, xr], writes=[ho])
                self.dma("ho%d" % (i % 2), t[dst][n * 128:(n + 1) * 128, t0 + nt * 512:t0 + (nt + 1) * 512], ho.ap[:, :], reads=[ho])

            self.linear_fm(lambda n, b: t[wd][n][:, b * 11:(b + 1) * 11, :].rearrange("p k j -> p (k j)"),
                           44, 4, 16, rhs_a, evac_d, ps[4:8], NT)
        self.phase_end()

    def evac_copy(self, i, out, in_, reads, writes):
        eng = "act" if i % 2 == 0 else "dve"
        return self.cast(eng, out, in_, reads, writes)

    def proj(self):
        T = 1024
        NT = T // 512
        t = self.t
        self.phase_begin()
        u = self.ba.take(16 * T)
        cl = self.ba.take(4 * T)
        sq_slots = [self.ba.take(T) for _ in range(2)]
        osb = [self.ba.take(512) for _ in range(3)]
        xs_slots = [self.fa.take(T) for _ in range(2)]
        rstd = self.fa.take(T)
        lat = self.fa.take(4 * T)
        Ct = self.fa.take(T)
        St = self.fa.take(T)
        self.wl_init()
        ps = self.ps
        cnt = {"o": 0}

        def out_bf(pb, dst):
            i = cnt["o"]
            cnt["o"] += 1
            o = osb[i % 3]
            self.evac_copy(i, o.ap[:, :], pb.ap[:, :], reads=[pb], writes=[o])
            self.dma("ob%d" % (i % 3), dst, o.ap[:, :], reads=[o])

        for tt_ in range(S // T):
            t0 = tt_ * T
            cs = slice(t0, t0 + T)
            col = lambda nt: slice(t0 + nt * 512, t0 + (nt + 1) * 512)
            self.rmsnorm_fm(lambda c: ("dram", t["h1T"][c * 128:(c + 1) * 128, cs]), 16, 16, u, T, D,
                            xs_slots, sq_slots, rstd, ps[4:4 + NT])
            self.dma("ropeC", Ct.ap[:, :], t["ropec"][0, :, cs], writes=[Ct])
            self.dma("ropeS", St.ap[:, :], t["ropec"][1, :, cs], writes=[St])
            rhs_u = lambda kc, nt: (u.ap[:, kc * T + nt * 512:kc * T + (nt + 1) * 512], u)
            rhs_cl = lambda kc, nt: (cl.ap[:, kc * T + nt * 512:kc * T + (nt + 1) * 512], cl)
            t1b, t2b = xs_slots[0], xs_slots[1]

            def ev_kr(n, si, nt, pb):
                c5 = slice(nt * 512, (nt + 1) * 512)
                if n == 0:
                    self.tt("dve", t1b.ap[:, c5], pb.ap[:, :], Ct.ap[:, c5], ALU.mult, reads=[pb, Ct], writes=[t1b])
                else:
                    self.tt("dve", t2b.ap[:, 0:512], pb.ap[:, :], St.ap[:, c5], ALU.mult, reads=[pb, St], writes=[t2b])
                    i = cnt["o"]
                    cnt["o"] += 1
                    o = osb[i % 3]
                    self.tt("dve", o.ap[:, :], t1b.ap[:, c5], t2b.ap[:, 0:512], ALU.add, reads=[t1b, t2b], writes=[o])
                    self.dma("ob%d" % (i % 3), t["kmr"][:, col(nt)], o.ap[:, :], reads=[o])

            def ev_qr(n, si, nt, pb):
                c5 = slice(nt * 512, (nt + 1) * 512)
                i = cnt["o"]
                cnt["o"] += 1
                o = osb[i % 3]
                self.tt("dve", o.ap[:, :], pb.ap[:, :], Ct.ap[:, c5], ALU.mult, reads=[pb, Ct], writes=[o])
                self.dma("ob%d" % (i % 3), t["qmr"][n][:, col(nt)], o.ap[:, :], reads=[o])

            def ev_lat(n, si, nt, pb):
                c = n % 4
                self.evac_copy(n + nt, lat.ap[:, c * T + nt * 512:c * T + (nt + 1) * 512], pb.ap[:, :], reads=[pb], writes=[lat])

            self.linear_fm(lambda n, b: t["w_qk"][n].rearrange("p k j -> p (k j)"), 16, 1, 16, rhs_u,
                           lambda n, si, nt, pb: out_bf(pb, (t["qna"][n] if n < 8 else t["kna"][n - 8])[:, col(nt)]),
                           ps[0:4], NT)
            self.linear_fm(lambda n, b: t["w_kr"][n].rearrange("p k j -> p (k j)"), 16, 1, 2, rhs_u, ev_kr, ps[0:4], NT)
            for half in range(NT):
                def ev_vna(cb, tc, pb, half=half):
                    r0 = t0 + half * 512 + tc * 128
                    out_bf(pb, t["vna"][r0:r0 + 128, cb * 512:(cb + 1) * 512])
                self.linear_tm(lambda cb, b: t["w_vna"][cb][:, b * 4:(b + 1) * 4, :].rearrange("p k j -> p (k j)"), 16, 2,
                               lambda kc, tc, half=half: (u.ap[:, kc * T + half * 512 + tc * 128:kc * T + half * 512 + (tc + 1) * 128], u),
                               4, ev_vna, ps[0:8])
            self.linear_fm(lambda n, b: t["w_lat"][n].rearrange("p k j -> p (k j)"), 16, 1, 4, rhs_u, ev_lat, ps[0:4], NT)
            self.rmsnorm_fm(lambda c: ("sb", lat.ap[:, c * T:(c + 1) * T], lat), 4, 80, cl, T, 512,
                            xs_slots, sq_slots, rstd, ps[4:4 + NT])
            self.linear_fm(lambda n, b: t["w_uqn"][n].rearrange("p k j -> p (k j)"), 4, 1, 8, rhs_cl,
                           lambda n, si, nt, pb: out_bf(pb, t["qmn"][n][:, col(nt)]), ps[0:4], NT)
            self.linear_fm(lambda n, b: t["w_uqr"][n].rearrange("p k j -> p (k j)"), 4, 1, 8, rhs_cl, ev_qr, ps[0:4], NT)
            self.linear_fm(lambda n, b: t["w_lat"][4 + n].rearrange("p k j -> p (k j)"), 16, 1, 4, rhs_u, ev_lat, ps[0:4], NT)
            self.rmsnorm_fm(lambda c: ("sb", lat.ap[:, c * T:(c + 1) * T], lat), 4, 84, cl, T, 512,
                            xs_slots, sq_slots, rstd, ps[4:4 + NT])
            self.linear_fm(lambda n, b: t["w_ukn"][n].rearrange("p k j -> p (k j)"), 4, 1, 8, rhs_cl,
                           lambda n, si, nt, pb: out_bf(pb, t["kmn"][n][:, col(nt)]), ps[0:4], NT)
            for half in range(NT):
                def ev_vm(cb, tc, pb, half=half):
                    r0 = t0 + half * 512 + tc * 128
                    out_bf(pb, t["vm"][r0:r0 + 128, cb * 512:(cb + 1) * 512])
                self.linear_tm(lambda cb, b: t["w_uv"][cb].rearrange("p k j -> p (k j)"), 4, 2,
                               lambda kc, tc, half=half: (cl.ap[:, kc * T + half * 512 + tc * 128:kc * T + half * 512 + (tc + 1) * 128], cl),
                               4, ev_vm, ps[0:8])
        self.phase_end()

    def attn_core(self, steps, load_head, S_mm, post_exp, scale, dst):
        ps = self.ps
        pexp = self.at_pexp
        nsteps = len(steps)
        sb = [ps[0], ps[1], ps[6], ps[7]]
        LOOK = 3

        def emit_S(i):
            h, qt, kc, first, last, extra = steps[i]
            S_mm(sb[i % 4], h, qt, kc)

        load_head(steps[0][0])
        for i in range(min(LOOK, nsteps)):
            emit_S(i)
        for i in range(nsteps):
            h, qt, kc, first, last, extra = steps[i]
            if i + LOOK < nsteps:
                emit_S(i + LOOK)
            pS = sb[i % 4]
            pe_ = pexp[i % len(pexp)]
            self.act(pe_.ap[:, :], pS.ap[:, :], AF.Exp, reads=[pS], writes=[pe_], scale=scale)
            pr = post_exp(i, pe_, h, extra)
            O = ps[2 + qt % 2]
            Dn = ps[4 + qt % 2]
            v = self.at_v[h % 2]
            self.mm(O, O.ap[:, :], v.ap[:, kc * 128:(kc + 1) * 128], pr.ap[:, :], first, last, [v, pr])
            self.mm(Dn, Dn.ap[:, :], self.ones.ap[:, :], pr.ap[:, :], first, last, [self.ones, pr])
            if (i == 0 or steps[i - 1][0] != h) and h + 1 < NH:
                load_head(h + 1)
            if last:
                rec = self.at_rec[qt % 2]
                o = self.at_o[qt % 2]
                self.P.op("dve", lambda e, rec=rec, Dn=Dn: e.reciprocal(out=rec.ap[:, :], in_=Dn.ap[:, :]), reads=[Dn], writes=[rec])
                self.tt("dve", o.ap[:, :], O.ap[:, :], rec.ap[:, :], ALU.mult, reads=[O, rec], writes=[o])
                self.dma("ao%d" % (qt % 2), self.t[dst][h][:, qt * 512:(qt + 1) * 512], o.ap[:, :], reads=[o])

    def mla(self):
        t = self.t
        self.phase_begin()
        qn = [self.ba.take(S) for _ in range(2)]
        qr = [self.ba.take(S) for _ in range(2)]
        kn = [self.ba.take(S) for _ in range(2)]
        self.at_v = [self.ba.take(S) for _ in range(2)]
        kr = self.ba.take(S)
        self.at_pexp = [self.ba.take(512) for _ in range(5)]
        self.at_o = [self.ba.take(512) for _ in range(2)]
        self.at_rec = [self.fa.take(512) for _ in range(2)]
        self.dma("kr", kr.ap[:, :], t["kmr"][:, :], writes=[kr])

        def load_head(h):
            s = h % 2
            self.dma("qn%d" % s, qn[s].ap[:, :], t["qmn"][h], writes=[qn[s]])
            self.dma("qr%d" % s, qr[s].ap[:, :], t["qmr"][h], writes=[qr[s]])
            self.dma("kn%d" % s, kn[s].ap[:, :], t["kmn"][h], writes=[kn[s]])
            self.dma("v%d" % s, self.at_v[s].ap[:, :].rearrange("p (c j) -> p c j", j=128),
                     t["vm"][:, h * 128:(h + 1) * 128].rearrange("(c p) j -> p c j", p=128), writes=[self.at_v[s]])

        def S_mm(pS, h, qt, kc):
            s = h % 2
            self.mm(pS, pS.ap[:, :], kn[s].ap[:, kc * 128:(kc + 1) * 128], qn[s].ap[:, qt * 512:(qt + 1) * 512], True, False, [kn[s], qn[s]])
            self.mm(pS, pS.ap[:, :], kr.ap[:, kc * 128:(kc + 1) * 128], qr[s].ap[:, qt * 512:(qt + 1) * 512], False, True, [kr, qr[s]])

        steps = [(h, qt, kc, kc == 0, kc == 15, None) for h in range(NH) for qt in range(4) for kc in range(16)]
        self.attn_core(steps, load_head, S_mm, lambda i, pe_, h, extra: pe_, 192 ** -0.5, "ob")
        self.phase_end()

    def na(self):
        t = self.t
        self.phase_begin()
        q = [self.ba.take(S) for _ in range(2)]
        k = [self.ba.take(S) for _ in range(2)]
        self.at_v = [self.ba.take(S) for _ in range(2)]
        E = [[self.ba.take(512) for _ in range(NG)] for _ in range(2)]
        Mb = [self.ba.take(512) for _ in range(NG)]
        self.at_pexp = [self.ba.take(512) for _ in range(5)]
        pm = [self.ba.take(512) for _ in range(5)]
        self.at_o = [self.ba.take(512) for _ in range(2)]
        self.at_rec = [self.fa.take(512) for _ in range(2)]
        est = [self.fa.take(512) for _ in range(3)]
        etmp = [self.fa.take(512) for _ in range(2)]
        cnt = {"e": 0}
        for g in range(NG):
            i = cnt["e"]
            cnt["e"] += 1
            self.dma("es%d" % (i % 3), est[i % 3].ap[:, :], t["nam"][g], writes=[est[i % 3]])
            self.cast("pool", Mb[g].ap[:, :], est[i % 3].ap[:, :], reads=[est[i % 3]], writes=[Mb[g]])

        def load_head(h):
            s = h % 2
            self.dma("qn%d" % s, q[s].ap[:, :], t["qna"][h], writes=[q[s]])
            self.dma("kn%d" % s, k[s].ap[:, :], t["kna"][h], writes=[k[s]])
            self.dma("v%d" % s, self.at_v[s].ap[:, :].rearrange("p (c j) -> p c j", j=128),
                     t["vna"][:, h * 128:(h + 1) * 128].rearrange("(c p) j -> p c j", p=128), writes=[self.at_v[s]])
            for g in range(NG):
                i = cnt["e"]
                cnt["e"] += 1
                es = est[i % 3]
                et = etmp[i % 2]
                self.dma("es%d" % (i % 3), es.ap[:, :], t["nag"][h, g], writes=[es])
                self.act(et.ap[:, :], es.ap[:, :], AF.Exp, reads=[es], writes=[et])
                self.tt("pool", E[s][g].ap[:, :], et.ap[:, :], Mb[g].ap[:, :], ALU.mult, reads=[et, Mb[g]], writes=[E[s][g]])

        def S_mm(pS, h, qt, j):
            s = h % 2
            self.mm(pS, pS.ap[:, :], k[s].ap[:, j * 128:(j + 1) * 128], q[s].ap[:, qt * 512:(qt + 1) * 512], True, True, [k[s], q[s]])

        def post_exp(i, pe_, h, g):
            o = pm[i % 5]
            eng = "dve"
            self.tt(eng, o.ap[:, :], pe_.ap[:, :], E[h % 2][g].ap[:, :], ALU.mult, reads=[pe_, E[h % 2][g]], writes=[o])
            return o

        steps = []
        for h in range(NH):
            for qt in range(4):
                tl = [(j, g) for (q_, j, g) in NA_TILES if q_ == qt]
                for ii, (j, g) in enumerate(tl):
                    steps.append((h, qt, j, ii == 0, ii == len(tl) - 1, g))
        self.attn_core(steps, load_head, S_mm, post_exp, 128 ** -0.5, "oa")
        self.phase_end()

    def merge(self):
        T = 1024
        NT = T // 512
        t = self.t
        self.phase_begin()
        u = self.ba.take(16 * T)
        oa = self.ba.take(8 * T)
        ob = self.ba.take(8 * T)
        mg = [[self.ba.take(512) for _ in range(NT)] for _ in range(16)]
        sq_slots = [self.ba.take(T) for _ in range(2)]
        xs_slots = [self.fa.take(T) for _ in range(2)]
        rstd = self.fa.take(T)
        sga = [self.fa.take(512) for _ in range(NT)]
        m1 = [self.fa.take(512) for _ in range(NT)]
        m2 = [self.fa.take(512) for _ in range(NT)]
        xres = [self.fa.take(512) for _ in range(2)]
        hout = [self.fa.take(512) for _ in range(2)]
        self.wl_init()
        ps = self.ps
        for tt_ in range(S // T):
            t0 = tt_ * T
            cs = slice(t0, t0 + T)
            self.rmsnorm_fm(lambda c: ("dram", t["h1T"][c * 128:(c + 1) * 128, cs]), 16, 16, u, T, D,
                            xs_slots, sq_slots, rstd, ps[4:4 + NT])
            for h in range(NH):
                self.dma("oa", oa.ap[:, h * T:(h + 1) * T], t["oa"][h][:, cs], writes=[oa])
            for h in range(NH):
                self.dma("ob", ob.ap[:, h * T:(h + 1) * T], t["ob"][h][:, cs], writes=[ob])
            rhs_u = lambda kc, nt: (u.ap[:, kc * T + nt * 512:kc * T + (nt + 1) * 512], u)
            rhs_oa = lambda kc, nt: (oa.ap[:, kc * T + nt * 512:kc * T + (nt + 1) * 512], oa)
            rhs_ob = lambda kc, nt: (ob.ap[:, kc * T + nt * 512:kc * T + (nt + 1) * 512], ob)
            jobs = []
            for n in range(16):
                def ev_g(si, nt, pb):
                    self.act(sga[nt].ap[:, :], pb.ap[:, :], AF.Sigmoid, reads=[pb], writes=[sga[nt]])

                def ev_ya(si, nt, pb):
                    self.tt("dve", m1[nt].ap[:, :], sga[nt].ap[:, :], pb.ap[:, :], ALU.mult, reads=[sga[nt], pb], writes=[m1[nt]])

                def ev_yb(si, nt, pb, n=n):
                    self.tt("dve", m2[nt].ap[:, :], sga[nt].ap[:, :], pb.ap[:, :], ALU.mult, reads=[sga[nt], pb], writes=[m2[nt]])
                    self.tt("pool", mg[n][nt].ap[:, :], m1[nt].ap[:, :], m2[nt].ap[:, :], ALU.add, reads=[m1[nt], m2[nt]], writes=[mg[n][nt]])

                jobs.append(dict(wsrc=lambda b, n=n: t["w_ga"][n].rearrange("p k j -> p (k j)"), nk=16, nblk=1, rhs=rhs_u, evac=ev_g))
                jobs.append(dict(wsrc=lambda b, n=n: t["w_a"][n].rearrange("p k j -> p (k j)"), nk=8, nblk=1, rhs=rhs_oa, evac=ev_ya))
                jobs.append(dict(wsrc=lambda b, n=n: t["w_gb"][n].rearrange("p k j -> p (k j)"), nk=16, nblk=1, rhs=rhs_u, evac=ev_g))
                jobs.append(dict(wsrc=lambda b, n=n: t["w_b"][n].rearrange("p k j -> p (k j)"), nk=8, nblk=1, rhs=rhs_ob, evac=ev_yb))
            self.linear_jobs(jobs, ps[0:4], NT)
            cnt = {"i": 0}

            def evac_o(n, si, nt, pb):
                i = cnt["i"]
                cnt["i"] += 1
                xr = xres[i % 2]
                ho = hout[i % 2]
                self.dma("xr%d" % (i % 2), xr.ap[:, :], t["h1T"][n * 128:(n + 1) * 128, t0 + nt * 512:t0 + (nt + 1) * 512], writes=[xr])
                self.tt("dve", ho.ap[:, :], pb.ap[:, :], xr.ap[:, :], ALU.add, reads=[pb, xr], writes=[ho])
                self.dma("ho%d" % (i % 2), t["h2T"][n * 128:(n + 1) * 128, t0 + nt * 512:t0 + (nt + 1) * 512], ho.ap[:, :], reads=[ho])

            self.linear_fm(lambda n, b: t["w_out"][n].rearrange("p k j -> p (k j)"), 16, 1, 16,
                           lambda kc, nt: (mg[kc][nt].ap[:, :], mg[kc][nt]), evac_o, ps[4:8], NT)
        self.phase_end()

    def pl(self):
        T = 512
        t = self.t
        self.phase_begin()
        v = self.ba.take(16 * T)
        pb16 = self.ba.take(2 * T)
        sq_slots = [self.ba.take(T) for _ in range(2)]
        xs_slots = [self.fa.take(T) for _ in range(2)]
        rstd = self.fa.take(T)
        h4 = [self.fa.take(T) for _ in range(16)]
        sgl = self.fa.take(T)
        self.wl_init()
        ps = self.ps
        for tt_ in range(S // T):
            t0 = tt_ * T
            cs = slice(t0, t0 + T)
            self.rmsnorm_fm(lambda c: ("dram", t["h3T"][c * 128:(c + 1) * 128, cs]), 16, 48, v, T, D,
                            xs_slots, sq_slots, rstd, ps[4:5])
            for c in range(2):
                xs = xs_slots[c]
                self.dma("xs%d" % c, xs.ap[:, :], t["pT"][c * 128:(c + 1) * 128, cs], writes=[xs])
                self.cast("pool", pb16.ap[:, c * T:(c + 1) * T], xs.ap[:, :], reads=[xs], writes=[pb16])
            jobs = []
            for n in range(16):
                def ev_g(si, nt, pb):
                    self.act(sgl.ap[:, :], pb.ap[:, :], AF.Sigmoid, reads=[pb], writes=[sgl])

                def ev_e(si, nt, pb, n=n):
                    xs = xs_slots[n % 2]
                    self.dma("xs%d" % (n % 2), xs.ap[:, :], t["h3T"][n * 128:(n + 1) * 128, cs], writes=[xs])
                    self.tt("dve", sgl.ap[:, :], sgl.ap[:, :], pb.ap[:, :], ALU.mult, reads=[sgl, pb], writes=[sgl])
                    self.tt("dve", h4[n].ap[:, :], sgl.ap[:, :], xs.ap[:, :], ALU.add, reads=[sgl, xs], writes=[h4[n]])

                jobs.append(dict(wsrc=lambda b, n=n: t["w_plg"][n].rearrange("p k j -> p (k j)"), nk=16, nblk=1,
                                 rhs=lambda kc, nt: (v.ap[:, kc * T:(kc + 1) * T], v), evac=ev_g))
                jobs.append(dict(wsrc=lambda b, n=n: t["w_pl"][n].rearrange("p k j -> p (k j)"), nk=2, nblk=1,
                                 rhs=lambda kc, nt: (pb16.ap[:, kc * T:(kc + 1) * T], pb16), evac=ev_e))
            self.linear_jobs(jobs, ps[0:2], 1)
            for c in range(16):
                sq = sq_slots[c % 2]
                self.act(sq.ap[:, :], h4[c].ap[:, :], AF.Square, reads=[h4[c]], writes=[sq])
                self.mm(ps[5], ps[5].ap[:, :], self.ones.ap[:, :], sq.ap[:, :], c == 0, c == 15, [sq, self.ones])
            self.act(rstd.ap[:, :], ps[5].ap[:, :], AF.Sqrt, reads=[ps[5], self.epsb], writes=[rstd], scale=1.0 / D,
                     bias=self.epsb.ap[:, 0:1])
            self.P.op("dve", lambda e: e.reciprocal(out=rstd.ap[:, :], in_=rstd.ap[:, :]), reads=[rstd], writes=[rstd])
            for c in range(16):
                self.stt("dve", h4[c].ap[:, :], h4[c].ap[:, :], self.gains.ap[:, 64 + c:65 + c], rstd.ap[:, :],
                         ALU.mult, ALU.mult, reads=[h4[c], rstd, self.gains], writes=[h4[c]])
                self.dma("out%d" % (c % 2), t["outT"][c * 128:(c + 1) * 128, cs], h4[c].ap[:, :], reads=[h4[c]])
            self.P.barrier()
        self.phase_end()

    def build(self):
        nc = self.nc
        P = self.P
        t = self.t
        with contextlib.ExitStack() as st:
            BCOLS = 71680
            FCOLS = 16384
            ba_t = st.enter_context(nc.sbuf_tensor("bf_arena", [128, BCOLS], BF16))
            fa_t = st.enter_context(nc.sbuf_tensor("f32_arena", [128, FCOLS], F32))
            self.ba = Arena(ba_t, BCOLS)
            self.fa = Arena(fa_t, FCOLS)
            self.ones = Buf(st.enter_context(nc.sbuf_tensor("ones_sb", [128, 128], BF16)))
            self.epsb = Buf(st.enter_context(nc.sbuf_tensor("epsb", [128, 1], F32)))
            self.gains = Buf(st.enter_context(nc.sbuf_tensor("gains_sb", [128, 88], F32)))
            self.ps = [Buf(st.enter_context(nc.psum_tensor("ps%d" % i, [128, 512], F32)), excl=True) for i in range(8)]
            P.op("dve", lambda e: e.memset(self.ones.ap[:, :], 1.0), writes=[self.ones])
            P.op("dve", lambda e: e.memset(self.epsb.ap[:, :], EPS), writes=[self.epsb])
            self.dma("gains", self.gains.ap[:, :], t["gains"][:, :], writes=[self.gains])
            if "ffn1" in self.phases:
                self.ffn("xT", "h1T", "wg1", "wu1", "wd1", 0)
            if "proj" in self.phases:
                self.proj()
            if "mla" in self.phases:
                self.mla()
            if "na" in self.phases:
                self.na()
            if "merge" in self.phases:
                self.merge()
            if "ffn2" in self.phases:
                self.ffn("h2T", "h3T", "wg2", "wu2", "wd2", 32)
            if "pl" in self.phases:
                self.pl()
            P.finish()
            P.emit()


def build_program(shared_shapes, phases=ALL_PHASES, debug=False):
    nc = bass.Bass("TRN2", target_bir_lowering=False)
    kb = KB(nc, shared_shapes, phases, debug)
    kb.build()
    return nc, kb


def kernel(**inputs):
    x = np.asarray(inputs["x"], dtype=np.float32)
    p = np.asarray(inputs["p"], dtype=np.float32)
    shared = prep_shared(inputs)
    nc, kb = build_program({k: v.shape for k, v in shared.items()})
    in_maps = []
    for b in range(8):
        m = dict(shared)
        m["xT"] = np.ascontiguousarray(x[b].T)
        m["pT"] = np.ascontiguousarray(p[0, b].T)
        in_maps.append(m)
    res = run_bass_kernel_spmd(nc, in_maps, core_ids=list(range(8)))
    out = np.stack([np.ascontiguousarray(r["outT"].T) for r in res.results], axis=0)
    return out.astype(np.float32)
```

```python
import contextlib
import numpy as np
import concourse.bass as bass
import concourse.mybir as mybir
from concourse.bass_utils import run_bass_kernel_spmd

F32 = mybir.dt.float32
BF16 = mybir.dt.bfloat16
AF = mybir.ActivationFunctionType
ALU = mybir.AluOpType

D = 2048
S = 2048
DFF = 5632
NH = 8
EPS = 1e-6
QNAMES = ("pe", "act", "dve", "pool", "sp")
ALL_PHASES = ("ffn1", "proj", "mla", "na", "merge", "ffn2", "pl")


class Buf:
    __slots__ = ("ap", "w", "r", "excl")

    def __init__(self, ap=None, excl=False):
        self.ap = ap
        self.w = []
        self.r = {}
        self.excl = excl


class Prog:
    def __init__(self, nc):
        self.nc = nc
        self.q = {n: [] for n in QNAMES}
        self.dma_count = {}
        self.pending = {n: [] for n in QNAMES}

    @staticmethod
    def _split(reads, writes):
        ex = [b for b in reads if b.excl]
        if ex:
            reads = [b for b in reads if not b.excl]
            writes = list(writes) + ex
        return reads, writes

    def _deps(self, reads, writes):
        deps = []
        for b in reads:
            deps += b.w
        for b in writes:
            deps += b.w
            deps += list(b.r.values())
        return deps

    def _mark(self, ev, reads, writes):
        for b in reads:
            if ev[0] == "e":
                b.r[ev[1]] = ev
            else:
                b.r[("d", ev[1])] = ev
        for b in writes:
            b.w = [ev]
            b.r = {}

    def op(self, q, fn, reads=(), writes=(), extra=()):
        reads, writes = self._split(reads, writes)
        deps = self._deps(reads, writes) + list(extra) + self.pending[q]
        self.pending[q] = []
        ev = ("e", q, len(self.q[q]))
        self.q[q].append(dict(fn=fn, deps=deps, signal=False, dma=None))
        self._mark(ev, reads, writes)
        return ev

    def dma(self, q, semkey, fn, reads=(), writes=(), extra=()):
        reads, writes = self._split(reads, writes)
        deps = self._deps(reads, writes) + list(extra) + self.pending[q]
        self.pending[q] = []
        c = self.dma_count.get(semkey, 0) + 16
        self.dma_count[semkey] = c
        ev = ("d", semkey, c)
        self.q[q].append(dict(fn=fn, deps=deps, signal=False, dma=semkey))
        self._mark(ev, reads, writes)
        return ev

    def barrier(self):
        evs = []
        for n in QNAMES:
            if self.q[n] and n != "sp":
                evs.append(("e", n, len(self.q[n]) - 1))
        for k, c in self.dma_count.items():
            evs.append(("d", k, c))
        for n in QNAMES:
            self.pending[n] = self.pending[n] + evs

    def finish(self):
        self.barrier()
        for n in QNAMES:
            if n != "sp":
                self.pending[n] = []
        self.op("sp", lambda e: e.nop())

    def emit(self):
        nc = self.nc
        for n in QNAMES:
            seen_e = {m: -1 for m in QNAMES}
            seen_d = {}
            for i, o in enumerate(self.q[n]):
                best = {}
                for ev in o["deps"]:
                    if ev[0] == "e":
                        _, m, idx = ev
                        if m == n and n in ("pe", "sp"):
                            continue
                        if idx <= seen_e[m]:
                            continue
                    else:
                        if ev[2] <= seen_d.get(ev[1], 0):
                            continue
                    key = (ev[0], ev[1])
                    if key not in best or ev[2] > best[key][2]:
                        best[key] = ev
                o["kept"] = list(best.values())
                for ev in o["kept"]:
                    if ev[0] == "e":
                        seen_e[ev[1]] = ev[2]
                        self.q[ev[1]][ev[2]]["signal"] = True
                    else:
                        seen_d[ev[1]] = ev[2]
        val = {}
        for n in QNAMES:
            c = 0
            v = []
            for o in self.q[n]:
                if o["signal"]:
                    c += 1
                v.append(c)
            val[n] = v
        self.stats = {n: (len(self.q[n]), val[n][-1] if val[n] else 0,
                          sum(len(o["kept"]) for o in self.q[n])) for n in QNAMES}
        with contextlib.ExitStack() as st:
            esem = {n: st.enter_context(nc.semaphore("s_" + n)) for n in QNAMES}
            dsem = {k: st.enter_context(nc.semaphore("d_%s" % (k,))) for k in self.dma_count}
            block = st.enter_context(nc.Block())

            def run(n, eng):
                for o in self.q[n]:
                    for ev in o["kept"]:
                        if ev[0] == "e":
                            eng.wait_ge(esem[ev[1]], val[ev[1]][ev[2]])
                        else:
                            eng.wait_ge(dsem[ev[1]], ev[2])
                    inst = o["fn"](eng)
                    if o["dma"] is not None:
                        inst.then_inc(dsem[o["dma"]], 16)
                    elif o["signal"]:
                        inst.then_inc(esem[n], 1)

            @block.tensor
            def _(eng):
                run("pe", eng)

            @block.scalar
            def _(eng):
                run("act", eng)

            @block.vector
            def _(eng):
                run("dve", eng)

            @block.gpsimd
            def _(eng):
                run("pool", eng)

            @block.sync
            def _(eng):
                run("sp", eng)


def lay(W, nb=128):
    K, N = W.shape
    return np.ascontiguousarray(W.reshape(K // 128, 128, N // nb, nb).transpose(2, 1, 0, 3))


def gain_cols(g):
    return np.ascontiguousarray(g.reshape(-1, 128).T)


def rot64(W):
    K, N = W.shape
    return W.reshape(K, N // 64, 2, 32)[:, :, ::-1, :].reshape(K, N)


def na_geometry():
    rows, Wd, kh, kw = 32, 64, 8, 16
    r = np.arange(rows)
    rs = np.clip(r - kh // 2, 0, rows - kh)
    c = np.arange(Wd)
    cs = np.clip(c - kw // 2, 0, Wd - kw)
    tiles = []
    geoms = {}
    glist = []
    for qt in range(4):
        qr = qt * 8 + np.arange(8)
        for j in range(16):
            kr = 2 * j + np.arange(2)
            rowok = (kr[:, None] >= rs[qr][None, :]) & (kr[:, None] < rs[qr][None, :] + kh)
            if not rowok.any():
                continue
            dr = kr[:, None] - qr[None, :] + (8 - 1)
            colok = (c[:, None] >= cs[None, :]) & (c[:, None] < cs[None, :] + kw)
            dc = np.clip(c[:, None] - c[None, :], -(kw - 1), kw - 1) + (kw - 1)
            mask = (rowok[:, None, :, None] & colok[None, :, None, :]).reshape(128, 512)
            dri = np.broadcast_to(np.clip(dr, 0, 14)[:, None, :, None], (2, 64, 8, 64)).reshape(128, 512)
            dci = np.broadcast_to(dc[None, :, None, :], (2, 64, 8, 64)).reshape(128, 512)
            key = (mask.tobytes(), (dri * mask).tobytes())
            if key not in geoms:
                geoms[key] = len(glist)
                glist.append((dri.copy(), dci.copy(), mask.copy()))
            tiles.append((qt, j, geoms[key]))
    return tiles, glist


NA_TILES, NA_GEOMS = na_geometry()
NG = len(NA_GEOMS)


def rope_tables():
    pos = np.arange(S, dtype=np.float32)
    inv = (1.0 / (np.float32(10000.0) ** (np.arange(0, 64, 2, dtype=np.float32) / np.float32(64)))).astype(np.float32)
    ang = (pos[:, None] * inv[None, :]).astype(np.float32)
    cos = np.cos(ang).astype(np.float32).T
    sin = np.sin(ang).astype(np.float32).T
    cs_ = np.concatenate([cos, cos, -sin, sin], 0)
    sc_ = np.concatenate([-sin, sin, cos, cos], 0)
    return np.ascontiguousarray(np.stack([cs_, sc_], 0))


def prep_shared(inp):
    w = {}
    f32 = lambda a: np.asarray(a, dtype=np.float32)
    w["gains"] = np.ascontiguousarray(np.concatenate([
        gain_cols(f32(inp["ffn1_norm"])[0]), gain_cols(f32(inp["mix_norm"])[0]),
        gain_cols(f32(inp["ffn2_norm"])[0]), gain_cols(f32(inp["pl_norm"])[0]),
        gain_cols(f32(inp["final_norm"])), gain_cols(f32(inp["q_a_norm"])[0]),
        gain_cols(f32(inp["kv_a_norm"])[0])], axis=1))
    w["wg1"] = lay(f32(inp["ffn1_w_gate"])[0])
    w["wu1"] = lay(f32(inp["ffn1_w_up"])[0])
    w["wd1"] = lay(f32(inp["ffn1_w_down"])[0])
    w["wg2"] = lay(f32(inp["ffn2_w_gate"])[0])
    w["wu2"] = lay(f32(inp["ffn2_w_up"])[0])
    w["wd2"] = lay(f32(inp["ffn2_w_down"])[0])
    win = f32(inp["w_in"])[0]
    w["w_qk"] = lay(win[:, 0:2048])
    w["w_vna"] = lay(win[:, 2048:3072], 512)
    w["w_lat"] = lay(win[:, 3072:4096])
    kr = win[:, 4096:4160]
    w["w_kr"] = lay(np.concatenate([kr, rot64(kr), rot64(kr), kr], axis=1))
    w["w_ga"] = lay(win[:, 4160:6208])
    w["w_gb"] = lay(win[:, 6208:8256])
    wuq = f32(inp["w_uq"])[0].reshape(512, NH, 192)
    w["w_uqn"] = lay(np.ascontiguousarray(wuq[:, :, :128]).reshape(512, NH * 128))
    qr = np.ascontiguousarray(wuq[:, :, 128:]).reshape(512, NH * 64)
    qrr = rot64(qr)
    w["w_uqr"] = lay(np.concatenate([qr.reshape(512, NH, 64), qrr.reshape(512, NH, 64)], axis=2).reshape(512, NH * 128))
    wukv = f32(inp["w_ukv"])[0].reshape(512, NH, 256)
    w["w_ukn"] = lay(np.ascontiguousarray(wukv[:, :, :128]).reshape(512, NH * 128))
    w["w_uv"] = lay(np.ascontiguousarray(wukv[:, :, 128:]).reshape(512, NH * 128), 512)
    w["w_a"] = lay(f32(inp["w_branch_a"])[0])
    w["w_b"] = lay(f32(inp["w_branch_b"])[0])
    w["w_out"] = lay(f32(inp["w_out"])[0])
    w["w_plg"] = lay(f32(inp["w_pl_gate"])[0])
    w["w_pl"] = lay(f32(inp["w_pl"])[0])
    w["ropec"] = rope_tables()
    rpb = f32(inp["na_rpb"])[0]
    dri = np.stack([g[0] for g in NA_GEOMS])
    dci = np.stack([g[1] for g in NA_GEOMS])
    w["nag"] = np.ascontiguousarray(rpb[:, dri, dci])
    w["nam"] = np.ascontiguousarray(np.stack([g[2] for g in NA_GEOMS]).astype(np.float32))
    return w


class Arena:
    def __init__(self, ap, ncols):
        self.ap = ap
        self.n = ncols
        self.off = 0

    def reset(self):
        self.off = 0

    def take(self, ncols, parts=128):
        assert self.off + ncols <= self.n, ("arena overflow", self.off, ncols, self.n)
        b = Buf(self.ap[0:parts, self.off:self.off + ncols])
        self.off += ncols
        return b


class KB:
    def __init__(self, nc, shared_shapes, phases, debug):
        self.nc = nc
        self.P = Prog(nc)
        self.keymap = {}
        self.phases = phases
        self.debug = debug
        self.t = {}
        dt = nc.dram_tensor
        self.t["xT"] = dt("xT", [D, S], F32, kind="ExternalInput").ap()
        self.t["pT"] = dt("pT", [256, S], F32, kind="ExternalInput").ap()
        for k, shp in shared_shapes.items():
            self.t[k] = dt(k, list(shp), F32, kind="ExternalInput").ap()
        self.t["outT"] = dt("outT", [D, S], F32, kind="ExternalOutput").ap()
        sk = "ExternalOutput" if debug else "Internal"
        for k in ("h1T", "h2T", "h3T"):
            self.t[k] = dt(k, [D, S], F32, kind=sk).ap()
        for k in ("qna", "kna", "qmn", "kmn", "oa", "ob"):
            self.t[k] = dt(k, [NH, 128, S], BF16, kind=sk).ap()
        self.t["qmr"] = dt("qmr", [NH, 128, S], BF16, kind=sk).ap()
        self.t["kmr"] = dt("kmr", [128, S], BF16, kind=sk).ap()
        self.t["vna"] = dt("vna", [S, NH * 128], BF16, kind=sk).ap()
        self.t["vm"] = dt("vm", [S, NH * 128], BF16, kind=sk).ap()

    def dma(self, key, out, in_, reads=(), writes=()):
        km = self.keymap
        if key not in km:
            km[key] = "k%d" % len(km)
        return self.P.dma("sp", km[key], lambda e: e.dma_start(out=out, in_=in_), reads=reads, writes=writes)

    def mm(self, ps, out, lhsT, rhs, start, stop, reads):
        return self.P.op("pe", lambda e: e.matmul(out, lhsT=lhsT, rhs=rhs, start=start, stop=stop),
                         reads=reads, writes=[ps])

    def act(self, out, in_, func, reads, writes, scale=None, bias=None):
        kw = {}
        if scale is not None:
            kw["scale"] = scale
        if bias is not None:
            kw["bias"] = bias
        return self.P.op("act", lambda e: e.activation(out=out, in_=in_, func=func, **kw), reads=reads, writes=writes)

    def cast(self, eng, out, in_, reads, writes):
        if eng == "act":
            return self.P.op("act", lambda e: e.activation(out=out, in_=in_, func=AF.Copy), reads=reads, writes=writes)
        return self.P.op(eng, lambda e: e.tensor_copy(out=out, in_=in_), reads=reads, writes=writes)

    def tt(self, eng, out, in0, in1, op, reads, writes):
        return self.P.op(eng, lambda e: e.tensor_tensor(out=out, in0=in0, in1=in1, op=op), reads=reads, writes=writes)

    def stt(self, eng, out, in0, scalar, in1, op0, op1, reads, writes):
        return self.P.op(eng, lambda e: e.scalar_tensor_tensor(out=out, in0=in0, scalar=scalar, in1=in1, op0=op0, op1=op1),
                         reads=reads, writes=writes)

    def wl_init(self, nstage=3, nwbf=4, cols=2048, cast_engs=("pool", "act")):
        self.wl_stage = [self.fa.take(cols) for _ in range(nstage)]
        self.wl_wbf = [self.ba.take(cols) for _ in range(nwbf)]
        self.wl_i = 0
        self.wl_j = 0
        self.wl_c = 0
        self.wl_engs = cast_engs

    def wl_load(self, src, n):
        si = self.wl_i % len(self.wl_stage)
        s = self.wl_stage[si]
        self.wl_i += 1
        b = self.wl_wbf[self.wl_j % len(self.wl_wbf)]
        self.wl_j += 1
        self.dma("st%d" % si, s.ap[:, :n], src, writes=[s])
        eng = self.wl_engs[self.wl_c % len(self.wl_engs)]
        self.wl_c += 1
        self.cast(eng, b.ap[:, :n], s.ap[:, :n], reads=[s], writes=[b])
        return b

    def linear_jobs(self, jobs, banks, NT, prefetch=2, ncols=512):
        seq = [(ji, b) for ji, j in enumerate(jobs) for b in range(j["nblk"])]
        loaded = {}
        st = {"nxt": 0}

        def ensure(i):
            while st["nxt"] <= min(i, len(seq) - 1):
                ji, b = seq[st["nxt"]]
                j = jobs[ji]
                loaded[st["nxt"]] = self.wl_load(j["wsrc"](b), (j["nk"] // j["nblk"]) * 128)
                st["nxt"] += 1

        for i, (ji, b) in enumerate(seq):
            ensure(i + prefetch)
            wb = loaded.pop(i)
            j = jobs[ji]
            subs = j.get("subs", ((0, 128),))
            nsub = len(subs)
            nk = j["nk"]
            kb = nk // j["nblk"]
            assert len(banks) >= 2 * NT * nsub
            for nt in range(NT):
                for si, (off, M) in enumerate(subs):
                    ps = banks[((ji % 2) * NT + nt) * nsub + si]
                    for k in range(kb):
                        kc = b * kb + k
                        rhs_ap, rhs_buf = j["rhs"](kc, nt)
                        self.mm(ps, ps.ap[0:M, 0:ncols], wb.ap[:, k * 128 + off:k * 128 + off + M], rhs_ap,
                                kc == 0, kc == nk - 1, [wb, rhs_buf])
                    if b == j["nblk"] - 1:
                        j["evac"](si, nt, ps)

    def linear_fm(self, wsrc, nk, nblk, n_out, rhs_fn, evac, banks, NT, subs=((0, 128),), prefetch=2):
        jobs = [dict(wsrc=(lambda b, n=n: wsrc(n, b)), nk=nk, nblk=nblk, rhs=rhs_fn,
                     evac=(lambda si, nt, ps, n=n: evac(n, si, nt, ps)), subs=subs) for n in range(n_out)]
        self.linear_jobs(jobs, banks, NT, prefetch)

    def linear_tm(self, wsrc, nk, n_cb, x_fn, ntc, evac, banks, kb=4):
        nblk = nk // kb
        seq = [(cb, b) for cb in range(n_cb) for b in range(nblk)]
        loaded = {0: self.wl_load(wsrc(*seq[0]), kb * 512)}
        for i, (cb, b) in enumerate(seq):
            if i + 1 < len(seq):
                loaded[i + 1] = self.wl_load(wsrc(*seq[i + 1]), kb * 512)
            wb = loaded.pop(i)
            for tc in range(ntc):
                ps = banks[(cb % 2) * ntc + tc]
                for k in range(kb):
                    kc = b * kb + k
                    x_ap, x_buf = x_fn(kc, tc)
                    self.mm(ps, ps.ap[:, :], x_ap, wb.ap[:, k * 512:(k + 1) * 512], kc == 0, kc == nk - 1, [wb, x_buf])
                if b == nblk - 1:
                    evac(cb, tc, ps)

    def rmsnorm_fm(self, src_fn, nch, gcol, xn, Tn, Dn, xs_slots, sq_slots, rstd, banks):
        NT = Tn // 512

        def fetch(c, tag):
            s = src_fn(c)
            if s[0] == "sb":
                return s[1], s[2]
            xs = xs_slots[c % len(xs_slots)]
            self.dma("xs%d" % (c % len(xs_slots)), xs.ap[:, :Tn], s[1], writes=[xs])
            return xs.ap[:, :Tn], xs

        for c in range(nch):
            x_ap, x_buf = fetch(c, 0)
            sq = sq_slots[c % len(sq_slots)]
            self.act(sq.ap[:, :Tn], x_ap, AF.Square, reads=[x_buf], writes=[sq])
            for nt in range(NT):
                self.mm(banks[nt], banks[nt].ap[:, :], self.ones.ap[:, :], sq.ap[:, nt * 512:(nt + 1) * 512],
                        c == 0, c == nch - 1, [sq, self.ones])
        for nt in range(NT):
            self.act(rstd.ap[:, nt * 512:(nt + 1) * 512], banks[nt].ap[:, :], AF.Sqrt, reads=[banks[nt], self.epsb],
                     writes=[rstd], scale=1.0 / Dn, bias=self.epsb.ap[:, 0:1])
        self.P.op("dve", lambda e: e.reciprocal(out=rstd.ap[:, :Tn], in_=rstd.ap[:, :Tn]), reads=[rstd], writes=[rstd])
        for c in range(nch):
            x_ap, x_buf = fetch(c, 1)
            self.stt("dve", xn.ap[:, c * Tn:(c + 1) * Tn], x_ap, self.gains.ap[:, gcol + c:gcol + c + 1], rstd.ap[:, :Tn],
                     ALU.mult, ALU.mult, reads=[x_buf, rstd, self.gains], writes=[xn])

    def phase_begin(self):
        self.ba.reset()
        self.fa.reset()
        self.keymap = {}

    def phase_end(self):
        self.P.barrier()

    def ffn(self, src, dst, wg, wu, wd, gcol):
        T = 1024
        NT = T // 512
        t = self.t
        self.phase_begin()
        xn = self.ba.take(16 * T)
        A = [[self.ba.take(512) for _ in range(NT)] for _ in range(44)]
        sq_slots = [self.ba.take(T) for _ in range(2)]
        xs_slots = [self.fa.take(T) for _ in range(4)]
        rstd = self.fa.take(T)
        sg = [self.fa.take(512) for _ in range(NT)]
        xres = [self.fa.take(512) for _ in range(2)]
        hout = [self.fa.take(512) for _ in range(2)]
        self.wl_init()
        ps = self.ps
        for tt_ in range(S // T):
            t0 = tt_ * T
            self.rmsnorm_fm(lambda c: ("dram", t[src][c * 128:(c + 1) * 128, t0:t0 + T]), 16, gcol, xn, T, D,
                            xs_slots, sq_slots, rstd, ps[4:4 + NT])

            def rhs_x(kc, nt):
                return xn.ap[:, kc * T + nt * 512:kc * T + (nt + 1) * 512], xn

            def evac_gu(n, si, nt, pb):
                f = n // 2
                if n % 2 == 0:
                    self.act(sg[nt].ap[:, :], pb.ap[:, :], AF.Silu, reads=[pb], writes=[sg[nt]])
                else:
                    self.tt("dve", A[f][nt].ap[:, :], sg[nt].ap[:, :], pb.ap[:, :], ALU.mult, reads=[sg[nt], pb], writes=[A[f][nt]])

            self.linear_fm(lambda n, b: (t[wg] if n % 2 == 0 else t[wu])[n // 2].rearrange("p k j -> p (k j)"),
                           16, 1, 88, rhs_x, evac_gu, ps[0:4], NT)

            def rhs_a(kc, nt):
                return A[kc][nt].ap[:, :], A[kc][nt]

            cnt = {"i": 0}

            def evac_d(n, si, nt, pb):
                i = cnt["i"]
                cnt["i"] += 1
                xr = xres[i % 2]
                ho = hout[i % 2]
                self.dma("xr%d" % (i % 2), xr.ap[:, :], t[src][n * 128:(n + 1) * 128, t0 + nt * 512:t0 + (nt + 1) * 512], writes=[xr])
                self.stt("dve", ho.ap[:, :], pb.ap[:, :], 0.5, xr.ap[:, :], ALU.mult, ALU.add, reads=[pb, xr], writes=[ho])
                self.dma("ho%d" % (i % 2), t[dst][n * 128:(n + 1) * 128, t0 + nt * 512:t0 + (nt + 1) * 512], ho.ap[:, :], reads=[ho])

            self.linear_fm(lambda n, b: t[wd][n][:, b * 11:(b + 1) * 11, :].rearrange("p k j -> p (k j)"),
                           44, 4, 16, rhs_a, evac_d, ps[4:8], NT)
        self.phase_end()

    def evac_copy(self, i, out, in_, reads, writes):
        eng = "act" if i % 2 == 0 else "dve"
        return self.cast(eng, out, in_, reads, writes)

    def proj(self):
        T = 1024
        NT = T // 512
        t = self.t
        self.phase_begin()
        u = self.ba.take(16 * T)
        cl = self.ba.take(4 * T)
        sq_slots = [self.ba.take(T) for _ in range(2)]
        osb = [self.ba.take(512) for _ in range(3)]
        xs_slots = [self.fa.take(T) for _ in range(3)]
        rstd = self.fa.take(T)
        lat = self.fa.take(4 * T)
        Ct = self.fa.take(T)
        St = self.fa.take(T)
        self.wl_init()
        ps = self.ps
        cnt = {"o": 0}

        def out_bf(pb, dst):
            i = cnt["o"]
            cnt["o"] += 1
            o = osb[i % 3]
            self.evac_copy(i, o.ap[:, :], pb.ap[:, :], reads=[pb], writes=[o])
            self.dma("ob%d" % (i % 3), dst, o.ap[:, :], reads=[o])

        for tt_ in range(S // T):
            t0 = tt_ * T
            cs = slice(t0, t0 + T)
            col = lambda nt: slice(t0 + nt * 512, t0 + (nt + 1) * 512)
            self.rmsnorm_fm(lambda c: ("dram", t["h1T"][c * 128:(c + 1) * 128, cs]), 16, 16, u, T, D,
                            xs_slots, sq_slots, rstd, ps[4:4 + NT])
            self.dma("ropeC", Ct.ap[:, :], t["ropec"][0, :, cs], writes=[Ct])
            self.dma("ropeS", St.ap[:, :], t["ropec"][1, :, cs], writes=[St])
            rhs_u = lambda kc, nt: (u.ap[:, kc * T + nt * 512:kc * T + (nt + 1) * 512], u)
            rhs_cl = lambda kc, nt: (cl.ap[:, kc * T + nt * 512:kc * T + (nt + 1) * 512], cl)
            t1b, t2b = xs_slots[0], xs_slots[1]

            def ev_kr(n, si, nt, pb):
                c5 = slice(nt * 512, (nt + 1) * 512)
                if n == 0:
                    self.tt("dve", t1b.ap[:, c5], pb.ap[:, :], Ct.ap[:, c5], ALU.mult, reads=[pb, Ct], writes=[t1b])
                else:
                    self.tt("dve", t2b.ap[:, 0:512], pb.ap[:, :], St.ap[:, c5], ALU.mult, reads=[pb, St], writes=[t2b])
                    i = cnt["o"]
                    cnt["o"] += 1
                    o = osb[i % 3]
                    self.tt("dve", o.ap[:, :], t1b.ap[:, c5], t2b.ap[:, 0:512], ALU.add, reads=[t1b, t2b], writes=[o])
                    self.dma("ob%d" % (i % 3), t["kmr"][:, col(nt)], o.ap[:, :], reads=[o])

            def ev_qr(n, si, nt, pb):
                c5 = slice(nt * 512, (nt + 1) * 512)
                i = cnt["o"]
                cnt["o"] += 1
                o = osb[i % 3]
                self.tt("dve", o.ap[:, :], pb.ap[:, :], Ct.ap[:, c5], ALU.mult, reads=[pb, Ct], writes=[o])
                self.dma("ob%d" % (i % 3), t["qmr"][n][:, col(nt)], o.ap[:, :], reads=[o])

            def ev_lat(n, si, nt, pb):
                c = n % 4
                self.evac_copy(n + nt, lat.ap[:, c * T + nt * 512:c * T + (nt + 1) * 512], pb.ap[:, :], reads=[pb], writes=[lat])

            self.linear_fm(lambda n, b: t["w_qk"][n].rearrange("p k j -> p (k j)"), 16, 1, 16, rhs_u,
                           lambda n, si, nt, pb: out_bf(pb, (t["qna"][n] if n < 8 else t["kna"][n - 8])[:, col(nt)]),
                           ps[0:4], NT)
            self.linear_fm(lambda n, b: t["w_kr"][n].rearrange("p k j -> p (k j)"), 16, 1, 2, rhs_u, ev_kr, ps[0:4], NT)
            for half in range(NT):
                def ev_vna(cb, tc, pb, half=half):
                    r0 = t0 + half * 512 + tc * 128
                    out_bf(pb, t["vna"][r0:r0 + 128, cb * 512:(cb + 1) * 512])
                self.linear_tm(lambda cb, b: t["w_vna"][cb][:, b * 4:(b + 1) * 4, :].rearrange("p k j -> p (k j)"), 16, 2,
                               lambda kc, tc, half=half: (u.ap[:, kc * T + half * 512 + tc * 128:kc * T + half * 512 + (tc + 1) * 128], u),
                               4, ev_vna, ps[0:8])
            self.linear_fm(lambda n, b: t["w_lat"][n].rearrange("p k j -> p (k j)"), 16, 1, 4, rhs_u, ev_lat, ps[0:4], NT)
            self.rmsnorm_fm(lambda c: ("sb", lat.ap[:, c * T:(c + 1) * T], lat), 4, 80, cl, T, 512,
                            xs_slots, sq_slots, rstd, ps[4:4 + NT])
            self.linear_fm(lambda n, b: t["w_uqn"][n].rearrange("p k j -> p (k j)"), 4, 1, 8, rhs_cl,
                           lambda n, si, nt, pb: out_bf(pb, t["qmn"][n][:, col(nt)]), ps[0:4], NT)
            self.linear_fm(lambda n, b: t["w_uqr"][n].rearrange("p k j -> p (k j)"), 4, 1, 8, rhs_cl, ev_qr, ps[0:4], NT)
            self.linear_fm(lambda n, b: t["w_lat"][4 + n].rearrange("p k j -> p (k j)"), 16, 1, 4, rhs_u, ev_lat, ps[0:4], NT)
            self.rmsnorm_fm(lambda c: ("sb", lat.ap[:, c * T:(c + 1) * T], lat), 4, 84, cl, T, 512,
                            xs_slots, sq_slots, rstd, ps[4:4 + NT])
            self.linear_fm(lambda n, b: t["w_ukn"][n].rearrange("p k j -> p (k j)"), 4, 1, 8, rhs_cl,
                           lambda n, si, nt, pb: out_bf(pb, t["kmn"][n][:, col(nt)]), ps[0:4], NT)
            for half in range(NT):
                def ev_vm(cb, tc, pb, half=half):
                    r0 = t0 + half * 512 + tc * 128
                    out_bf(pb, t["vm"][r0:r0 + 128, cb * 512:(cb + 1) * 512])
                self.linear_tm(lambda cb, b: t["w_uv"][cb].rearrange("p k j -> p (k j)"), 4, 2,
                               lambda kc, tc, half=half: (cl.ap[:, kc * T + half * 512 + tc * 128:kc * T + half * 512 + (tc + 1) * 128], cl),
                               4, ev_vm, ps[0:8])
        self.phase_end()

    def attn_core(self, steps, load_head, S_mm, post_exp, scale, dst, per_step=None):
        ps = self.ps
        pexp = self.at_pexp
        nsteps = len(steps)
        sb = [ps[0], ps[1], ps[6], ps[7]]
        LOOK = 3

        def emit_S(i):
            h, qt, kc, first, last, extra = steps[i]
            S_mm(sb[i % 4], h, qt, kc)

        hstart = 0
        load_head(steps[0][0])
        for i in range(min(LOOK, nsteps)):
            emit_S(i)
        for i in range(nsteps):
            h, qt, kc, first, last, extra = steps[i]
            if i + LOOK < nsteps:
                emit_S(i + LOOK)
            pS = sb[i % 4]
            pe_ = pexp[i % len(pexp)]
            self.act(pe_.ap[:, :], pS.ap[:, :], AF.Exp, reads=[pS], writes=[pe_], scale=scale)
            pr = post_exp(i, pe_, h, extra)
            O = ps[2 + qt % 2]
            Dn = ps[4 + qt % 2]
            v = self.at_v[h % 2]
            self.mm(O, O.ap[:, :], v.ap[:, kc * 128:(kc + 1) * 128], pr.ap[:, :], first, last, [v, pr])
            self.mm(Dn, Dn.ap[:, :], self.ones.ap[:, :], pr.ap[:, :], first, last, [self.ones, pr])
            if i == 0 or steps[i - 1][0] != h:
                hstart = i
                if h + 1 < NH:
                    load_head(h + 1)
            if per_step is not None:
                per_step(h, i - hstart)
            if last:
                rec = self.at_rec[qt % 2]
                o = self.at_o[qt % 2]
                self.P.op("dve", lambda e, rec=rec, Dn=Dn: e.reciprocal(out=rec.ap[:, :], in_=Dn.ap[:, :]), reads=[Dn], writes=[rec])
                self.tt("dve", o.ap[:, :], O.ap[:, :], rec.ap[:, :], ALU.mult, reads=[O, rec], writes=[o])
                self.dma("ao%d" % (qt % 2), self.t[dst][h][:, qt * 512:(qt + 1) * 512], o.ap[:, :], reads=[o])

    def mla(self):
        t = self.t
        self.phase_begin()
        qn = [self.ba.take(S) for _ in range(2)]
        qr = [self.ba.take(S) for _ in range(2)]
        kn = [self.ba.take(S) for _ in range(2)]
        self.at_v = [self.ba.take(S) for _ in range(2)]
        kr = self.ba.take(S)
        self.at_pexp = [self.ba.take(512) for _ in range(5)]
        self.at_o = [self.ba.take(512) for _ in range(2)]
        self.at_rec = [self.fa.take(512) for _ in range(2)]
        self.dma("kr", kr.ap[:, :], t["kmr"][:, :], writes=[kr])

        def load_head(h):
            s = h % 2
            self.dma("qn%d" % s, qn[s].ap[:, :], t["qmn"][h], writes=[qn[s]])
            self.dma("qr%d" % s, qr[s].ap[:, :], t["qmr"][h], writes=[qr[s]])
            self.dma("kn%d" % s, kn[s].ap[:, :], t["kmn"][h], writes=[kn[s]])
            self.dma("v%d" % s, self.at_v[s].ap[:, :].rearrange("p (c j) -> p c j", j=128),
                     t["vm"][:, h * 128:(h + 1) * 128].rearrange("(c p) j -> p c j", p=128), writes=[self.at_v[s]])

        def S_mm(pS, h, qt, kc):
            s = h % 2
            self.mm(pS, pS.ap[:, :], kn[s].ap[:, kc * 128:(kc + 1) * 128], qn[s].ap[:, qt * 512:(qt + 1) * 512], True, False, [kn[s], qn[s]])
            self.mm(pS, pS.ap[:, :], kr.ap[:, kc * 128:(kc + 1) * 128], qr[s].ap[:, qt * 512:(qt + 1) * 512], False, True, [kr, qr[s]])

        steps = [(h, qt, kc, kc == 0, kc == 15, None) for h in range(NH) for qt in range(4) for kc in range(16)]
        self.attn_core(steps, load_head, S_mm, lambda i, pe_, h, extra: pe_, 192 ** -0.5, "ob")
        self.phase_end()

    def na(self):
        t = self.t
        self.phase_begin()
        q = [self.ba.take(S) for _ in range(2)]
        k = [self.ba.take(S) for _ in range(2)]
        self.at_v = [self.ba.take(S) for _ in range(2)]
        E = [[self.ba.take(512) for _ in range(NG)] for _ in range(2)]
        Mb = [self.ba.take(512) for _ in range(NG)]
        self.at_pexp = [self.ba.take(512) for _ in range(5)]
        pm = [self.ba.take(512) for _ in range(5)]
        self.at_o = [self.ba.take(512) for _ in range(2)]
        self.at_rec = [self.fa.take(512) for _ in range(2)]
        est = [self.fa.take(512) for _ in range(NG)]
        mst = [self.fa.take(512) for _ in range(2)]
        etmp = [self.fa.take(512) for _ in range(2)]
        cnt = {"e": 0}
        for g in range(NG):
            self.dma("ms%d" % (g % 2), mst[g % 2].ap[:, :], t["nam"][g], writes=[mst[g % 2]])
            self.cast("pool", Mb[g].ap[:, :], mst[g % 2].ap[:, :], reads=[mst[g % 2]], writes=[Mb[g]])

        def build_E(h, g):
            i = cnt["e"]
            cnt["e"] += 1
            et = etmp[i % 2]
            self.act(et.ap[:, :], est[g].ap[:, :], AF.Exp, reads=[est[g]], writes=[et])
            self.tt("pool", E[h % 2][g].ap[:, :], et.ap[:, :], Mb[g].ap[:, :], ALU.mult, reads=[et, Mb[g]], writes=[E[h % 2][g]])

        def load_head(h):
            s = h % 2
            self.dma("qn%d" % s, q[s].ap[:, :], t["qna"][h], writes=[q[s]])
            self.dma("kn%d" % s, k[s].ap[:, :], t["kna"][h], writes=[k[s]])
            self.dma("v%d" % s, self.at_v[s].ap[:, :].rearrange("p (c j) -> p c j", j=128),
                     t["vna"][:, h * 128:(h + 1) * 128].rearrange("(c p) j -> p c j", p=128), writes=[self.at_v[s]])
            for g in range(NG):
                self.dma("es%d" % g, est[g].ap[:, :], t["nag"][h, g], writes=[est[g]])
            if h == 0:
                for g in range(NG):
                    build_E(0, g)

        def per_step(h, j):
            if h + 1 < NH and j < NG:
                build_E(h + 1, j)

        def S_mm(pS, h, qt, j):
            s = h % 2
            self.mm(pS, pS.ap[:, :], k[s].ap[:, j * 128:(j + 1) * 128], q[s].ap[:, qt * 512:(qt + 1) * 512], True, True, [k[s], q[s]])

        def post_exp(i, pe_, h, g):
            o = pm[i % 5]
            eng = "dve"
            self.tt(eng, o.ap[:, :], pe_.ap[:, :], E[h % 2][g].ap[:, :], ALU.mult, reads=[pe_, E[h % 2][g]], writes=[o])
            return o

        steps = []
        for h in range(NH):
            for qt in range(4):
                tl = [(j, g) for (q_, j, g) in NA_TILES if q_ == qt]
                for ii, (j, g) in enumerate(tl):
                    steps.append((h, qt, j, ii == 0, ii == len(tl) - 1, g))
        self.attn_core(steps, load_head, S_mm, post_exp, 128 ** -0.5, "oa", per_step=per_step)
        self.phase_end()

    def merge(self):
        T = 1024
        NT = T // 512
        t = self.t
        self.phase_begin()
        u = self.ba.take(16 * T)
        oa = self.ba.take(8 * T)
        ob = self.ba.take(8 * T)
        mg = [[self.ba.take(512) for _ in range(NT)] for _ in range(16)]
        sq_slots = [self.ba.take(T) for _ in range(2)]
        xs_slots = [self.fa.take(T) for _ in range(4)]
        rstd = self.fa.take(T)
        sga = [self.fa.take(512) for _ in range(NT)]
        m1 = [self.fa.take(512) for _ in range(NT)]
        m2 = [self.fa.take(512) for _ in range(NT)]
        xres = [self.fa.take(512) for _ in range(2)]
        hout = [self.fa.take(512) for _ in range(2)]
        self.wl_init()
        ps = self.ps
        for tt_ in range(S // T):
            t0 = tt_ * T
            cs = slice(t0, t0 + T)
            self.rmsnorm_fm(lambda c: ("dram", t["h1T"][c * 128:(c + 1) * 128, cs]), 16, 16, u, T, D,
                            xs_slots, sq_slots, rstd, ps[4:4 + NT])
            for h in range(NH):
                self.dma("oa", oa.ap[:, h * T:(h + 1) * T], t["oa"][h][:, cs], writes=[oa])
            for h in range(NH):
                self.dma("ob", ob.ap[:, h * T:(h + 1) * T], t["ob"][h][:, cs], writes=[ob])
            rhs_u = lambda kc, nt: (u.ap[:, kc * T + nt * 512:kc * T + (nt + 1) * 512], u)
            rhs_oa = lambda kc, nt: (oa.ap[:, kc * T + nt * 512:kc * T + (nt + 1) * 512], oa)
            rhs_ob = lambda kc, nt: (ob.ap[:, kc * T + nt * 512:kc * T + (nt + 1) * 512], ob)
            jobs = []
            for n in range(16):
                def ev_g(si, nt, pb):
                    self.act(sga[nt].ap[:, :], pb.ap[:, :], AF.Sigmoid, reads=[pb], writes=[sga[nt]])

                def ev_ya(si, nt, pb):
                    self.tt("dve", m1[nt].ap[:, :], sga[nt].ap[:, :], pb.ap[:, :], ALU.mult, reads=[sga[nt], pb], writes=[m1[nt]])

                def ev_yb(si, nt, pb, n=n):
                    self.tt("dve", m2[nt].ap[:, :], sga[nt].ap[:, :], pb.ap[:, :], ALU.mult, reads=[sga[nt], pb], writes=[m2[nt]])
                    self.tt("pool", mg[n][nt].ap[:, :], m1[nt].ap[:, :], m2[nt].ap[:, :], ALU.add, reads=[m1[nt], m2[nt]], writes=[mg[n][nt]])

                jobs.append(dict(wsrc=lambda b, n=n: t["w_ga"][n].rearrange("p k j -> p (k j)"), nk=16, nblk=1, rhs=rhs_u, evac=ev_g))
                jobs.append(dict(wsrc=lambda b, n=n: t["w_a"][n].rearrange("p k j -> p (k j)"), nk=8, nblk=1, rhs=rhs_oa, evac=ev_ya))
                jobs.append(dict(wsrc=lambda b, n=n: t["w_gb"][n].rearrange("p k j -> p (k j)"), nk=16, nblk=1, rhs=rhs_u, evac=ev_g))
                jobs.append(dict(wsrc=lambda b, n=n: t["w_b"][n].rearrange("p k j -> p (k j)"), nk=8, nblk=1, rhs=rhs_ob, evac=ev_yb))
            self.linear_jobs(jobs, ps[0:4], NT)
            cnt = {"i": 0}

            def evac_o(n, si, nt, pb):
                i = cnt["i"]
                cnt["i"] += 1
                xr = xres[i % 2]
                ho = hout[i % 2]
                self.dma("xr%d" % (i % 2), xr.ap[:, :], t["h1T"][n * 128:(n + 1) * 128, t0 + nt * 512:t0 + (nt + 1) * 512], writes=[xr])
                self.tt("dve", ho.ap[:, :], pb.ap[:, :], xr.ap[:, :], ALU.add, reads=[pb, xr], writes=[ho])
                self.dma("ho%d" % (i % 2), t["h2T"][n * 128:(n + 1) * 128, t0 + nt * 512:t0 + (nt + 1) * 512], ho.ap[:, :], reads=[ho])

            self.linear_fm(lambda n, b: t["w_out"][n].rearrange("p k j -> p (k j)"), 16, 1, 16,
                           lambda kc, nt: (mg[kc][nt].ap[:, :], mg[kc][nt]), evac_o, ps[4:8], NT)
        self.phase_end()

    def pl(self):
        T = 512
        t = self.t
        self.phase_begin()
        v = self.ba.take(16 * T)
        pb16 = self.ba.take(2 * T)
        sq_slots = [self.ba.take(T) for _ in range(2)]
        xs_slots = [self.fa.take(T) for _ in range(2)]
        rstd = self.fa.take(T)
        h4 = [self.fa.take(T) for _ in range(16)]
        sgl = self.fa.take(T)
        self.wl_init()
        ps = self.ps
        for tt_ in range(S // T):
            t0 = tt_ * T
            cs = slice(t0, t0 + T)
            self.rmsnorm_fm(lambda c: ("dram", t["h3T"][c * 128:(c + 1) * 128, cs]), 16, 48, v, T, D,
                            xs_slots, sq_slots, rstd, ps[4:5])
            for c in range(2):
                xs = xs_slots[c]
                self.dma("xs%d" % c, xs.ap[:, :], t["pT"][c * 128:(c + 1) * 128, cs], writes=[xs])
                self.cast("pool", pb16.ap[:, c * T:(c + 1) * T], xs.ap[:, :], reads=[xs], writes=[pb16])
            jobs = []
            for n in range(16):
                def ev_g(si, nt, pb):
                    self.act(sgl.ap[:, :], pb.ap[:, :], AF.Sigmoid, reads=[pb], writes=[sgl])

                def ev_e(si, nt, pb, n=n):
                    xs = xs_slots[n % 2]
                    self.dma("xs%d" % (n % 2), xs.ap[:, :], t["h3T"][n * 128:(n + 1) * 128, cs], writes=[xs])
                    self.tt("dve", sgl.ap[:, :], sgl.ap[:, :], pb.ap[:, :], ALU.mult, reads=[sgl, pb], writes=[sgl])
                    self.tt("dve", h4[n].ap[:, :], sgl.ap[:, :], xs.ap[:, :], ALU.add, reads=[sgl, xs], writes=[h4[n]])

                jobs.append(dict(wsrc=lambda b, n=n: t["w_plg"][n].rearrange("p k j -> p (k j)"), nk=16, nblk=1,
                                 rhs=lambda kc, nt: (v.ap[:, kc * T:(kc + 1) * T], v), evac=ev_g))
                jobs.append(dict(wsrc=lambda b, n=n: t["w_pl"][n].rearrange("p k j -> p (k j)"), nk=2, nblk=1,
                                 rhs=lambda kc, nt: (pb16.ap[:, kc * T:(kc + 1) * T], pb16), evac=ev_e))
            self.linear_jobs(jobs, ps[0:2], 1)
            for c in range(16):
                sq = sq_slots[c % 2]
                self.act(sq.ap[:, :], h4[c].ap[:, :], AF.Square, reads=[h4[c]], writes=[sq])
                self.mm(ps[5], ps[5].ap[:, :], self.ones.ap[:, :], sq.ap[:, :], c == 0, c == 15, [sq, self.ones])
            self.act(rstd.ap[:, :], ps[5].ap[:, :], AF.Sqrt, reads=[ps[5], self.epsb], writes=[rstd], scale=1.0 / D,
                     bias=self.epsb.ap[:, 0:1])
            self.P.op("dve", lambda e: e.reciprocal(out=rstd.ap[:, :], in_=rstd.ap[:, :]), reads=[rstd], writes=[rstd])
            for c in range(16):
                self.stt("dve", h4[c].ap[:, :], h4[c].ap[:, :], self.gains.ap[:, 64 + c:65 + c], rstd.ap[:, :],
                         ALU.mult, ALU.mult, reads=[h4[c], rstd, self.gains], writes=[h4[c]])
                self.dma("out%d" % (c % 2), t["outT"][c * 128:(c + 1) * 128, cs], h4[c].ap[:, :], reads=[h4[c]])
            self.P.barrier()
        self.phase_end()

    def build(self):
        nc = self.nc
        P = self.P
        t = self.t
        with contextlib.ExitStack() as st:
            BCOLS = 71680
            FCOLS = 16384
            ba_t = st.enter_context(nc.sbuf_tensor("bf_arena", [128, BCOLS], BF16))
            fa_t = st.enter_context(nc.sbuf_tensor("f32_arena", [128, FCOLS], F32))
            self.ba = Arena(ba_t, BCOLS)
            self.fa = Arena(fa_t, FCOLS)
            self.ones = Buf(st.enter_context(nc.sbuf_tensor("ones_sb", [128, 128], BF16)))
            self.epsb = Buf(st.enter_context(nc.sbuf_tensor("epsb", [128, 1], F32)))
            self.gains = Buf(st.enter_context(nc.sbuf_tensor("gains_sb", [128, 88], F32)))
            self.ps = [Buf(st.enter_context(nc.psum_tensor("ps%d" % i, [128, 512], F32)), excl=True) for i in range(8)]
            P.op("dve", lambda e: e.memset(self.ones.ap[:, :], 1.0), writes=[self.ones])
            P.op("dve", lambda e: e.memset(self.epsb.ap[:, :], EPS), writes=[self.epsb])
            self.dma("gains", self.gains.ap[:, :], t["gains"][:, :], writes=[self.gains])
            if "ffn1" in self.phases:
                self.ffn("xT", "h1T", "wg1", "wu1", "wd1", 0)
            if "proj" in self.phases:
                self.proj()
            if "mla" in self.phases:
                self.mla()
            if "na" in self.phases:
                self.na()
            if "merge" in self.phases:
                self.merge()
            if "ffn2" in self.phases:
                self.ffn("h2T", "h3T", "wg2", "wu2", "wd2", 32)
            if "pl" in self.phases:
                self.pl()
            P.finish()
            P.emit()


def build_program(shared_shapes, phases=ALL_PHASES, debug=False):
    nc = bass.Bass("TRN2", target_bir_lowering=False)
    kb = KB(nc, shared_shapes, phases, debug)
    kb.build()
    return nc, kb


def kernel(**inputs):
    x = np.asarray(inputs["x"], dtype=np.float32)
    p = np.asarray(inputs["p"], dtype=np.float32)
    shared = prep_shared(inputs)
    nc, kb = build_program({k: v.shape for k, v in shared.items()})
    in_maps = []
    for b in range(8):
        m = dict(shared)
        m["xT"] = np.ascontiguousarray(x[b].T)
        m["pT"] = np.ascontiguousarray(p[0, b].T)
        in_maps.append(m)
    res = run_bass_kernel_spmd(nc, in_maps, core_ids=list(range(8)))
    out = np.stack([np.ascontiguousarray(r["outT"].T) for r in res.results], axis=0)
    return out.astype(np.float32)
```
